# Optimizing a Trainium2 kernel written in Bass

```python
import math
import jax, jax.numpy as jnp
from jax import lax
import numpy as np

D_MODEL = 1024
BATCH = 8
SEQ = 2048
DEPTH = 1
DEC_BATCH = 128
DEC_SEQ = 4
PAST_LEN = 16384
PAGE_SIZE = 128

MIX_WIDTH = D_MODEL
A_WIDTH = MIX_WIDTH // 2
B_WIDTH = MIX_WIDTH - A_WIDTH
A_HEAD_DIM = 128
A_KEY_DIM = 128
A_HEADS = A_WIDTH // A_HEAD_DIM
A_QK = A_HEADS * A_KEY_DIM
CONV_WIDTH = 31
CHUNK = 64
EPS = 1e-6
IN_WIDTH = A_QK + A_QK + A_WIDTH + A_WIDTH + 2 * B_WIDTH + B_WIDTH

kernel_name = "hymba_hgrn2_conformer_conv_step"


def rmsnorm(x, g):
    xf = x.astype(jnp.float32)
    y = xf * lax.rsqrt(jnp.mean(xf * xf, axis=-1, keepdims=True) + EPS)
    return (y * g.astype(jnp.float32)).astype(x.dtype)


def layernorm(x, g, b):
    xf = x.astype(jnp.float32)
    mu = jnp.mean(xf, axis=-1, keepdims=True)
    var = jnp.mean(jnp.square(xf - mu), axis=-1, keepdims=True)
    y = (xf - mu) * lax.rsqrt(var + EPS)
    return (y * g.astype(jnp.float32) + b.astype(jnp.float32)).astype(x.dtype)


def hgrn2_chunked(q, k, v, logf, s0):
    n, t, h, dk = q.shape
    dv = v.shape[-1]
    c = math.gcd(t, CHUNK)
    nc = t // c

    def to_chunks(a):
        return a.astype(jnp.float32).reshape(n, nc, c, h, a.shape[-1]).transpose(1, 0, 3, 2, 4)

    qc, kc, vc, gc = to_chunks(q), to_chunks(k), to_chunks(v), to_chunks(logf)
    causal = jnp.tril(jnp.ones((c, c), dtype=bool))[:, :, None]

    def step(s, inp):
        qb, kb, vb, gb = inp
        g = jnp.cumsum(gb, axis=2)
        diff = g[:, :, :, None, :] - g[:, :, None, :, :]
        decay = jnp.exp(jnp.where(causal, diff, -jnp.inf))
        scores = jnp.einsum('nhtd,nhsd,nhtsd->nhts', qb, kb, decay)
        o = (jnp.einsum('nhts,nhsv->nhtv', scores, vb)
             + jnp.einsum('nhtd,nhdv->nhtv', qb * jnp.exp(g), s))
        g_last = g[:, :, -1:, :]
        s_new = (jnp.exp(g_last[:, :, 0, :])[..., None] * s
                 + jnp.einsum('nhsd,nhsv->nhdv', kb * jnp.exp(g_last - g), vb))
        return s_new, o

    s_fin, oc = lax.scan(step, s0.astype(jnp.float32), (qc, kc, vc, gc))
    o = oc.transpose(1, 0, 3, 2, 4).reshape(n, t, h, dv)
    return o, s_fin


def causal_dwconv(u, buf, w, b):
    full = jnp.concatenate([buf.astype(u.dtype), u], axis=1)
    y = lax.conv_general_dilated(full, w.astype(u.dtype)[:, None, :], (1,), 'VALID',
                                 dimension_numbers=('NWC', 'WIO', 'NWC'),
                                 feature_group_count=u.shape[-1])
    return y + b.astype(u.dtype), full[:, -(CONV_WIDTH - 1):]


def mixer_layer(x, s0, conv_buf, norm_g, w_in, lb, a_norm_g, b_glu, conv_w, conv_b, ln_g, ln_b, w_out):
    n, t, _ = x.shape
    h = rmsnorm(x, norm_g)
    proj = h @ w_in.astype(h.dtype)
    s1 = A_QK
    s2 = s1 + A_QK
    s3 = s2 + A_WIDTH
    s4 = s3 + A_WIDTH
    s5 = s4 + 2 * B_WIDTH
    q, fp, i, za, glu, zb = jnp.split(proj, [s1, s2, s3, s4, s5], axis=-1)
    q = jax.nn.silu(q)
    lbf = lb.astype(jnp.float32)
    f = lbf + (1.0 - lbf) * jax.nn.sigmoid(fp.astype(jnp.float32))
    k = 1.0 - f
    logf = jnp.log(f)
    heads = lambda a, d: a.reshape(n, t, A_HEADS, d)
    o_a, s_new = hgrn2_chunked(heads(q, A_KEY_DIM), heads(k, A_KEY_DIM),
                               heads(i, A_HEAD_DIM), heads(logf, A_KEY_DIM), s0)
    o_a = rmsnorm(o_a, a_norm_g).astype(x.dtype).reshape(n, t, A_WIDTH) * jax.nn.silu(za)
    glu = glu + b_glu.astype(glu.dtype)
    u = glu[..., :B_WIDTH] * jax.nn.sigmoid(glu[..., B_WIDTH:])
    cv, buf_new = causal_dwconv(u, conv_buf, conv_w, conv_b)
    o_b = jax.nn.silu(layernorm(cv, ln_g, ln_b)) * jax.nn.silu(zb)
    out = jnp.concatenate([o_a, o_b], axis=-1) @ w_out.astype(x.dtype)
    return x + out, s_new, buf_new


def setup_inputs(seed: int = 0) -> dict:
    key = jax.random.key(seed)
    ks = jax.random.split(key, 16)
    f32 = jnp.float32
    return {
        "x_prompt": jax.random.normal(ks[0], (BATCH, SEQ, D_MODEL), f32),
        "x_sample": jax.random.normal(ks[1], (DEC_BATCH, DEC_SEQ, D_MODEL), f32),
        "state_hgrn": 0.5 * jax.random.normal(ks[2], (DEPTH, DEC_BATCH, A_HEADS, A_KEY_DIM, A_HEAD_DIM), f32),
        "state_conv": 0.5 * jax.random.normal(ks[3], (DEPTH, DEC_BATCH, CONV_WIDTH - 1, B_WIDTH), f32),
        "norm_in_g": 1.0 + 0.05 * jax.random.normal(ks[4], (DEPTH, D_MODEL), f32),
        "w_in": jax.random.normal(ks[5], (DEPTH, D_MODEL, IN_WIDTH), f32) * D_MODEL ** -0.5,
        "lb_logits": 0.5 * jax.random.normal(ks[6], (DEPTH + 1, A_QK), f32),
        "hgrn_norm_g": 1.0 + 0.05 * jax.random.normal(ks[7], (DEPTH, A_HEAD_DIM), f32),
        "b_glu": 0.02 * jax.random.normal(ks[8], (DEPTH, 2 * B_WIDTH), f32),
        "conv_w": jax.random.normal(ks[9], (DEPTH, CONV_WIDTH, B_WIDTH), f32) * CONV_WIDTH ** -0.5,
        "conv_b": 0.02 * jax.random.normal(ks[10], (DEPTH, B_WIDTH), f32),
        "ln_g": 1.0 + 0.05 * jax.random.normal(ks[11], (DEPTH, B_WIDTH), f32),
        "ln_b": 0.02 * jax.random.normal(ks[12], (DEPTH, B_WIDTH), f32),
        "w_out": jax.random.normal(ks[13], (DEPTH, MIX_WIDTH, D_MODEL), f32) * MIX_WIDTH ** -0.5,
        "final_norm_g": 1.0 + 0.05 * jax.random.normal(ks[14], (D_MODEL,), f32),
    }


def reference(x_prompt, x_sample, state_hgrn, state_conv, norm_in_g, w_in, lb_logits, hgrn_norm_g,
              b_glu, conv_w, conv_b, ln_g, ln_b, w_out, final_norm_g):
    lb_all = jnp.cumsum(jax.nn.softmax(lb_logits.astype(jnp.float32), axis=0), axis=0)
    hp, hs = x_prompt, x_sample
    sp_list, cp_list, ss_list, cs_list = [], [], [], []
    for l in range(DEPTH):
        lw = (norm_in_g[l], w_in[l], lb_all[l], hgrn_norm_g[l], b_glu[l], conv_w[l], conv_b[l],
              ln_g[l], ln_b[l], w_out[l])
        s0p = jnp.zeros((hp.shape[0], A_HEADS, A_KEY_DIM, A_HEAD_DIM), jnp.float32)
        c0p = jnp.zeros((hp.shape[0], CONV_WIDTH - 1, B_WIDTH), hp.dtype)
        hp, sp, cp = mixer_layer(hp, s0p, c0p, *lw)
        hs, ss, cs = mixer_layer(hs, state_hgrn[l], state_conv[l], *lw)
        sp_list.append(sp)
        cp_list.append(cp)
        ss_list.append(ss)
        cs_list.append(cs)
    y_prompt = rmsnorm(hp, final_norm_g)
    y_sample = rmsnorm(hs, final_norm_g)
    new_state_hgrn_prompt = jnp.stack(sp_list)
    new_state_conv_prompt = jnp.stack(cp_list)
    new_state_hgrn_sample = jnp.stack(ss_list)
    new_state_conv_sample = jnp.stack(cs_list)
    return (y_prompt, y_sample, new_state_hgrn_prompt, new_state_conv_prompt, new_state_hgrn_sample, new_state_conv_sample)
```

```python
import numpy as np
from contextlib import ExitStack
import ml_dtypes
import concourse.bass as bass
import concourse.mybir as mybir
from concourse.bass_utils import run_bass_kernel_spmd

F32 = mybir.dt.float32
BF16 = mybir.dt.bfloat16
ALU = mybir.AluOpType
AF = mybir.ActivationFunctionType

NCORES = 8
D = 1024
SEQ = 2048
TB = 256
NBLK = SEQ // TB
NSS = 16
TS = NSS * 4
INW = 3584
EPS = 1e-6
BIG = 3.0e38
SAME_ENG_SYNC = ('act', 'dve', 'pool')


class Buf:
    def __init__(self, name, excl=False):
        self.name = name
        self.excl = excl
        self.w = {}
        self.r = {}
        self.dsem = None
        self.dcnt = 0


class FW:
    def __init__(self, nc, es):
        self.nc = nc
        self.es = es
        self.engs = {'pe': nc.tensor, 'act': nc.scalar, 'dve': nc.vector,
                     'pool': nc.gpsimd, 'sp': nc.sync}
        self.cnt = {}
        self.seen = {}
        self.semobj = {}
        self.dcount = {}
        for k in self.engs:
            self.semobj[k] = es.enter_context(nc.semaphore('s_' + k))
            self.cnt[k] = 0
            self.seen[k] = {}
        self.ndsem = 0

    def sb(self, name, shape, dt=F32, es=None):
        return (es or self.es).enter_context(self.nc.sbuf_tensor(name, list(shape), dt))

    def ps(self, name, shape, dt=F32):
        return self.es.enter_context(self.nc.psum_tensor(name, list(shape), dt))

    def _wait(self, eng, key, val):
        if key == eng and eng not in SAME_ENG_SYNC:
            return
        if self.seen[eng].get(key, 0) >= val:
            return
        self.engs[eng].wait_ge(self.semobj[key], val)
        self.seen[eng][key] = val

    def _deps(self, eng, reads, writes):
        need = {}
        for b in reads:
            for k, v in b.w.items():
                need[k] = max(need.get(k, 0), v)
        for b in writes:
            for k, v in b.w.items():
                need[k] = max(need.get(k, 0), v)
            for k, v in b.r.items():
                need[k] = max(need.get(k, 0), v)
        for k, v in need.items():
            self._wait(eng, k, v)

    def _commit(self, key, val, reads, writes):
        for b in writes:
            if b.r:
                b.w = {}
                b.r = {}
            b.w[key] = max(b.w.get(key, 0), val)
        for b in reads:
            b.r[key] = max(b.r.get(key, 0), val)

    def op(self, eng, fn, *args, reads=(), writes=(), **kw):
        def _flat(xs):
            out = []
            for x in xs:
                if isinstance(x, (list, tuple)):
                    out.extend(_flat(x))
                else:
                    out.append(x)
            return out
        reads = _flat(reads)
        writes = _flat(writes)
        writes = list(writes) + [b for b in reads if b.excl and b not in writes]
        reads = [b for b in reads if not b.excl]
        self._deps(eng, reads, writes)
        ins = fn(*args, **kw)
        if eng == 'pe' and kw.get('stop') is False:
            self._commit(eng, self.cnt[eng] + 1, reads, writes)
            return ins
        self.cnt[eng] += 1
        ins.then_inc(self.semobj[eng], 1)
        self._commit(eng, self.cnt[eng], reads, writes)
        return ins

    def dma(self, eng, out, in_, tag, reads=(), writes=(), nowaw=False, **kw):
        if nowaw and tag.dsem is not None:
            saved = [(b, b.w.pop(tag.dsem, None)) for b in writes]
            self._deps(eng, reads, writes)
            for b, v in saved:
                if v is not None:
                    b.w[tag.dsem] = v
        else:
            self._deps(eng, reads, writes)
        if tag.dsem is None:
            tag.dsem = 'd%d' % self.ndsem
            self.ndsem += 1
            self.semobj[tag.dsem] = self.es.enter_context(self.nc.semaphore(tag.dsem))
        ins = self.engs[eng].dma_start(out=out, in_=in_, **kw)
        tag.dcnt += 1
        ins.then_inc(self.semobj[tag.dsem], 16)
        self.dcount[tag.dsem] = tag.dcnt * 16
        self._commit(tag.dsem, tag.dcnt * 16, reads, writes)
        return ins

    def wait_all(self, eng, bufs):
        self._deps(eng, bufs, bufs)

    def barrier(self):
        for e in self.engs:
            for k in self.engs:
                if k != e and self.cnt[k] > 0:
                    self._wait(e, k, self.cnt[k])
            for k, v in self.dcount.items():
                self._wait(e, k, v)


class _Stop(Exception):
    pass


DEBUG_STOP = None
DG_ENG = None


def build_nc():
    nc = bass.Bass("TRN2", target_bir_lowering=False)
    try:
        _build(nc)
    except _Stop:
        pass
    return nc


def _build(nc):

    def din(name, shape, dt=F32):
        return nc.dram_tensor(name, list(shape), dt, kind="ExternalInput").ap()

    def dout(name, shape, dt=F32):
        return nc.dram_tensor(name, list(shape), dt, kind="ExternalOutput").ap()

    x_p = din("x_p", [SEQ, D])
    x_s = din("x_s", [TS, D])
    st_h = din("st_h", [NSS, 4, 128, 128])
    st_c = din("st_c", [NSS, 30, 512])
    norm_in_g = din("norm_in_g", [8, 128])
    w_in = din("w_in", [D, INW])
    lb_logits = din("lb_logits", [8, 128])
    hgrn_norm_g = din("hgrn_norm_g", [1, 128])
    b_glu = din("b_glu", [8, 128])
    conv_w = din("conv_w", [31, 512])
    conv_b = din("conv_b", [4, 128])
    ln_g = din("ln_g", [4, 128])
    ln_b = din("ln_b", [4, 128])
    w_out = din("w_out", [D, D])
    final_g = din("final_g", [1, D])
    c_idb = din("c_idb", [128, 128], BF16)
    c_idf = din("c_idf", [128, 128])
    c_maskp = din("c_maskp", [128, 256])
    c_masks = din("c_masks", [128, 256])
    c_mcol = din("c_mcol", [64, 16])

    y_p = dout("y_p", [SEQ, D])
    y_s = dout("y_s", [TS, D])
    sh_p = dout("sh_p", [4, 128, 128])
    sc_p = dout("sc_p", [30, 512])
    sh_s = dout("sh_s", [NSS, 4, 128, 128])
    sc_s = dout("sc_s", [NSS, 30, 512])

    with ExitStack() as es:
        fw = FW(nc, es)
        op = fw.op
        V, A_, G, PE_ = nc.vector, nc.scalar, nc.gpsimd, nc.tensor
        out_bufs = []

        def ck(name):
            if DEBUG_STOP == name:
                fw.barrier()
                raise _Stop()

        class TT:
            def __init__(self, name, shape, dt=F32, scope=None):
                self.t = fw.sb(name, shape, dt, es=scope)
                self.b = Buf(name)

        A0 = fw.ps("A0", [128, 1024]); A1 = fw.ps("A1", [128, 1024])
        OT = fw.ps("OT", [128, 1024]); BOT = Buf("OT", True)
        PP = fw.ps("PP", [128, 512]); BPP = Buf("PP", True)
        MISC = fw.ps("MISC", [128, 512]); BSC = Buf("SC", True); BTRB = BSC
        SC = MISC[:, 0:256]
        TRB = MISC[:, 256:512].bitcast(BF16)
        slots = [(A0, Buf("A0", True)), (A1, Buf("A1", True))]
        ia = [0]

        def getA():
            s = slots[ia[0] % 2]
            ia[0] += 1
            return s

        w_in_bf = fw.sb("w_in_bf", [128, 8, INW], BF16)
        Bwin = [Buf("win%d" % g) for g in range(7)]
        w_out_bf = fw.sb("w_out_bf", [128, 8, D], BF16); Bwout = Buf("wout")
        WK = fw.sb("WK", [128, 16, 8, 32], BF16); BWKa = Buf("WKa"); BWKb = Buf("WKb")
        wsc = TT("wsc", [128, 4, 4, 8])
        cwTb = TT("cwTb", [128, 4, 32], BF16)
        idrep = TT("idrep", [128, 32], BF16)
        Ue = TT("Ue", [128, 4, 31 + TB], BF16)
        UBk = TT("UBk", [128, 16, TB + 28], BF16)
        gfin = TT("gfin", [128, D])
        idb = TT("idb", [128, 128], BF16)
        idf = TT("idf", [128, 128])
        maskp = TT("maskp", [128, 256])
        masks = TT("masks", [128, 256])
        mcol = TT("mcol", [64, 16])
        prm = TT("prm", [128, 40])
        cwT = TT("cwT", [128, 4, 32])
        on128 = TT("on128", [128, 128], BF16)
        on512 = TT("on512", [128, 128], BF16)
        on512f = TT("on512f", [128, 128])
        cst = TT("cst", [128, 8])
        omlb = TT("omlb", [128, 8])

        XT = fw.sb("XT", [128, 6, D]); Bx = [Buf("x%d" % i) for i in range(6)]
        xs = TT("xs", [128, 2, D], BF16)
        hT = TT("hT", [128, 8, TB], BF16)
        W = [TT("W%d" % i, [128, 4, TB]) for i in range(6)]
        NGp = TT("NGp", [128, 4, TB + 1])
        qT = TT("qT", [128, 4, TB], BF16)
        kT = TT("kT", [128, 4, TB], BF16)
        keT = TT("keT", [128, 4, TB], BF16)
        vtok = TT("vtok", [128, 2, 512], BF16)
        ktok = TT("ktok", [128, 2, 512], BF16)
        zas = TT("zas", [128, 4, TB], BF16)
        zbs = TT("zbs", [128, 4, TB], BF16)
        dd = TT("dd", [128, 3, 4, 16])
        ee = TT("ee", [128, 3, 4, 16])
        ssb = TT("ssb", [128, 512], BF16)
        sqa = TT("sqa", [128, 4, TB], BF16)
        cvy = TT("cvy", [128, 4, TB], BF16)
        sml = TT("sml", [128, TB])
        rsl = TT("rsl", [128, TB])
        mean_sb = TT("mean_sb", [128, TB])
        oT = TT("oT", [128, 8, TB], BF16)
        utok = TT("utok", [64, 512])
        stt_ = TT("stt_", [128, 16])
        prow = TT("prow", [40, 128])
        onerow = TT("onerow", [1, 128])

        def load(tt, src, eng='sp'):
            fw.dma(eng, tt.t[:], src, tt.b, writes=[tt.b])

        load(idb, c_idb[:, :]); load(idf, c_idf[:, :])
        rows = [(lb_logits, 0, 8), (norm_in_g, 8, 8), (b_glu, 16, 8), (conv_b, 24, 4),
                (ln_g, 28, 4), (ln_b, 32, 4), (hgrn_norm_g, 36, 1)]
        op('pool', G.memset, prow.t[:], 0.0, writes=[prow.b])
        for (src, r0, n) in rows:
            fw.dma('sp', prow.t[r0:r0 + n, :], src[:, :], prow.b, writes=[prow.b])
        class _V:
            pass
        cwrow = _V(); cwrow.t = XT[0:31, 2, 0:512]; cwrow.b = Bx[2]
        gfrow = _V(); gfrow.t = XT[0:1, 3, :]; gfrow.b = Bx[3]
        fw.dma('sp', cwrow.t, conv_w[:, :], cwrow.b, writes=[cwrow.b])
        fw.dma('sp', gfrow.t, final_g[:, :], gfrow.b, writes=[gfrow.b])
        load(maskp, c_maskp[:, :]); load(masks, c_masks[:, :]); load(mcol, c_mcol[:, :])
        op('pool', G.memset, cst.t[:, 0:1], 1.0, writes=[cst.b])
        op('pool', G.memset, cst.t[:, 1:2], -0.5, writes=[cst.b])
        op('pool', G.memset, cst.t[:, 2:3], EPS, writes=[cst.b])
        op('pool', G.memset, on128.t[:], 1.0 / 128, writes=[on128.b])
        op('pool', G.memset, on512.t[:], 1.0 / 512, writes=[on512.b])
        op('pool', G.memset, on512f.t[:], 1.0 / 512, writes=[on512f.b])
        op('pool', G.memset, onerow.t[:], 1.0, writes=[onerow.b])
        op('pool', G.memset, NGp.t[:, :, 0:1], 0.0, writes=[NGp.b])

        fw.dma('sp', XT[0:TS, 0, :], x_s[:, :], Bx[0], writes=[Bx[0]])

        ck('su0')
        Aps, BA = getA()
        op('pe', PE_.transpose, Aps[:, 0:40], prow.t[:, :], idf.t[0:40, 0:40], reads=[prow.b, idf.b], writes=[BA])
        op('act', A_.copy, prm.t[:], Aps[:, 0:40], reads=[BA], writes=[prm.b])
        Aps, BA = getA()
        for ct in range(4):
            op('pe', PE_.transpose, Aps[:, ct * 32:ct * 32 + 31], XT[0:31, 2, ct * 128:(ct + 1) * 128],
               idf.t[0:31, 0:31], reads=[cwrow.b, idf.b], writes=[BA])
        op('pool', G.memset, cwT.t[:, :, 0:1], 0.0, writes=[cwT.b])
        op('act', A_.copy, cwT.t[:, :, 1:32], Aps[:, 0:128].rearrange("p (c j) -> p c j", j=32)[:, :, 0:31],
           reads=[BA], writes=[cwT.b])
        op('act', A_.copy, cwTb.t[:], cwT.t[:], reads=[cwT.b], writes=[cwTb.b])
        op('dve', V.tensor_tensor, idrep.t[:], idb.t[:, 0:32], idb.t[:, 32:64], ALU.add, reads=[idb.b], writes=[idrep.b])
        op('dve', V.tensor_tensor, idrep.t[:], idrep.t[:], idb.t[:, 64:96], ALU.add, reads=[idb.b, idrep.b], writes=[idrep.b])
        op('dve', V.tensor_tensor, idrep.t[:], idrep.t[:], idb.t[:, 96:128], ALU.add, reads=[idb.b, idrep.b], writes=[idrep.b])
        Apw, BAw = getA()
        for r in range(4):
            for j in range(4):
                kw = {'tile_position': (0, 96)} if j == 3 else {}
                op('pe', PE_.matmul, Apw[32 * j:32 * j + 32, r * 32:(r + 1) * 32].rearrange("p (c g) -> p c g", c=4),
                   idb.t[:, 32 * r:32 * r + 32],
                   cwTb.t[:, :, :].rearrange("p c (g j) -> p c g j", j=4)[:, :, :, j], start=True, stop=True,
                   reads=[idb.b, cwTb.b], writes=[BAw], **kw)
        op('act', A_.copy, wsc.t[:].rearrange("p r c g -> p (r c g)"), Apw[:, 0:128], reads=[BAw], writes=[wsc.b])
        ck('su1')
        op('dve', V.tensor_tensor, omlb.t[:, 4:8], prm.t[:, 4:8], prm.t[:, 0:4], ALU.subtract,
           reads=[prm.b], writes=[omlb.b])
        op('act', A_.activation, omlb.t[:, 0:4], omlb.t[:, 4:8], AF.Sigmoid, reads=[omlb.b], writes=[omlb.b])
        op('act', A_.mul, omlb.t[:, 4:8], omlb.t[:, 0:4], -1.0, reads=[omlb.b], writes=[omlb.b])
        ck('su2')
        Aps, BA = getA()
        for hh in range(2):
            op('pe', PE_.matmul, Aps[:, hh * 512:(hh + 1) * 512], onerow.t[0:1, :], XT[0:1, 3, hh * 512:(hh + 1) * 512],
               start=True, stop=True, reads=[onerow.b, gfrow.b], writes=[BA])
        op('act', A_.copy, gfin.t[:], Aps[:, :], reads=[BA], writes=[gfin.b])

        def build_dg(lo, hi):
            for cg in range(lo // 18 * 3, min(16, (hi // 18) * 3)):
                ct, r = cg // 4, cg % 4
                e = 'dve' if cg % 2 == 0 else 'pool'
                op(e, fw.engs[e].tensor_tensor, WK[:, cg, :, :],
                   idrep.t[:, :].rearrange("p (o c) -> p o c", o=1).to_broadcast([128, 8, 32]),
                   wsc.t[:, r, ct, :].rearrange("p (g o) -> p g o", o=1).to_broadcast([128, 8, 32]), ALU.mult,
                   reads=[idrep.b, wsc.b], writes=[BWKa if e == 'dve' else BWKb])
        ck('su3')
        stag = [([W[i] for i in range(4)]), None]
        cast_engs = ['dve', 'pool', 'act']
        ci = 0
        for g in range(7):
            if g % 2 == 0:
                parts = [(W[i].t[:].rearrange("p a b -> p (a b)"), W[i].b) for i in range(4)]
            else:
                parts = [(XT[:, 2, :], Bx[2]), (XT[:, 3, :], Bx[3]),
                         (W[4].t[:].rearrange("p a b -> p (a b)"), W[4].b),
                         (W[5].t[:].rearrange("p a b -> p (a b)"), W[5].b)]
            for pi, (pap, pb) in enumerate(parts):
                src = w_in[pi * 256:(pi + 1) * 256, g * 512:(g + 1) * 512].rearrange("(k p) f -> p k f", p=128)
                fw.dma('sp', pap.rearrange("p (k f) -> p k f", k=2), src, pb, writes=[pb])
                for kk in range(2):
                    k = 2 * pi + kk
                    e = cast_engs[ci % 3]; ci += 1
                    dst = w_in_bf[:, k, g * 512:(g + 1) * 512]
                    srcv = pap[:, kk * 512:(kk + 1) * 512]
                    if e == 'act':
                        op('act', A_.activation, dst, srcv, AF.Copy, scale=prm.t[:, 8 + k:9 + k],
                           reads=[pb, prm.b], writes=[Bwin[g]])
                    elif e == 'pool':
                        op(e, G.tensor_scalar, dst, srcv, prm.t[:, 8 + k:9 + k], 1.0, ALU.mult, ALU.mult,
                           reads=[pb, prm.b], writes=[Bwin[g]])
                    else:
                        op(e, fw.engs[e].tensor_scalar, dst, srcv, prm.t[:, 8 + k:9 + k], None, ALU.mult,
                           reads=[pb, prm.b], writes=[Bwin[g]])
            build_dg(g * 18, (g + 1) * 18)
        ck('su4')
        for k in range(8):
            if k % 2 == 0:
                pap, pb = W[k // 2 % 4].t[:].rearrange("p a b -> p (a b)"), W[k // 2 % 4].b
            else:
                pap, pb = (XT[:, 2 + (k // 2) % 2, :], Bx[2 + (k // 2) % 2])
            fw.dma('sp', pap, w_out[k * 128:(k + 1) * 128, :], pb, writes=[pb])
            e = cast_engs[ci % 3]; ci += 1
            if e == 'act':
                op('act', A_.copy, w_out_bf[:, k, :], pap, reads=[pb], writes=[Bwout])
            else:
                op(e, fw.engs[e].tensor_copy, w_out_bf[:, k, :], pap, reads=[pb], writes=[Bwout])
        ck('su5')
        ck('setup')
        Bxs = [Buf("xs0"), Buf("xs1")]
        BstA = [Buf("stA0"), Buf("stA1")]
        BstE = Buf("stE")

        def stage_A1(T, tiles):
            for i, (xap, bx, P) in enumerate(tiles):
                op('act', A_.activation, xs.t[0:P, i, :], xap, AF.Square, accum_out=stt_.t[0:P, i:i + 1],
                   reads=[bx], writes=[Bxs[i], BstA[i]])
                op('pool', G.tensor_scalar, stt_.t[0:P, 4 + i:5 + i], stt_.t[0:P, i:i + 1], 1.0 / D, EPS, ALU.mult, ALU.add,
                   reads=[BstA[i]], writes=[BstA[i]])
                op('pool', G.tensor_tensor, stt_.t[0:P, 8 + i:9 + i], stt_.t[0:P, 4 + i:5 + i], cst.t[0:P, 1:2], ALU.pow,
                   reads=[BstA[i], cst.b], writes=[BstA[i]])
                op('pool', G.tensor_scalar, xs.t[0:P, i, :], xap, stt_.t[0:P, 8 + i:9 + i], 1.0, ALU.mult, ALU.mult,
                   reads=[bx, BstA[i]], writes=[Bxs[i]])

        def stage_A2(T, tiles):
            Aps, BA = getA()
            Abf = Aps[:].bitcast(BF16).rearrange("p (k t) -> p k t", k=8)
            for i, (xap, bx, P) in enumerate(tiles):
                for k in range(8):
                    op('pe', PE_.transpose, Abf[:, k, i * 128:i * 128 + P], xs.t[0:P, i, k * 128:(k + 1) * 128],
                       idb.t[0:P, 0:P], reads=[Bxs[i], idb.b], writes=[BA])
            op('act', A_.copy, hT.t[:, :, 0:T], Abf[:, :, 0:T], reads=[BA], writes=[hT.b])

        def stage_A(T, tiles):
            stage_A1(T, tiles)
            stage_A2(T, tiles)

        def inproj_fm(T, g):
            Aps, BA = getA()
            Av = Aps[:, 0:4 * T].rearrange("p (j t) -> p j t", j=4)
            for j in range(4):
                for k in range(8):
                    op('pe', PE_.matmul, Av[:, j, :], w_in_bf[:, k, g * 512 + j * 128:g * 512 + (j + 1) * 128],
                       hT.t[:, k, 0:T], start=(k == 0), stop=(k == 7), reads=[Bwin[g], hT.b], writes=[BA])
            return Av, BA

        def grp(name, T, ntile, P):
            ufp, sg = W[4], W[5]
            if name == 'f':
                Av, BA = inproj_fm(T, 1)
                op('act', A_.activation, W[1].t[:, :, 0:T], Av, AF.Sigmoid, scale=-1.0, reads=[BA], writes=[W[1].b])
            elif name == 'q':
                Av, BA = inproj_fm(T, 0)
                op('act', A_.activation, W[0].t[:, :, 0:T], Av, AF.Silu, reads=[BA], writes=[W[0].b])
            elif name == 'v':
                Aps, BA = getA()
                for i in range(ntile):
                    for k in range(8):
                        op('pe', PE_.matmul, Aps[0:P, i * 512:(i + 1) * 512], hT.t[:, k, i * 128:i * 128 + P],
                           w_in_bf[:, k, 1024:1536], start=(k == 0), stop=(k == 7), reads=[Bwin[2], hT.b], writes=[BA])
                op('dve', V.tensor_copy, vtok.t[0:P, 0:ntile, :], Aps[0:P, 0:ntile * 512].rearrange("p (i f) -> p i f", i=ntile),
                   reads=[BA], writes=[vtok.b])
            elif name == 'za':
                Av, BA = inproj_fm(T, 3)
                op('act', A_.activation, zas.t[:, :, 0:T], Av, AF.Silu, reads=[BA], writes=[zas.b])
            elif name == 'gate':
                Av, BA = inproj_fm(T, 5)
                for ct in range(4):
                    op('act', A_.activation, sg.t[:, ct, 0:T], Av[:, ct, :], AF.Sigmoid, bias=prm.t[:, 20 + ct:21 + ct],
                       reads=[BA, prm.b], writes=[sg.b])
            elif name == 'a':
                Av, BA = inproj_fm(T, 4)
                for ct in range(4):
                    op('dve', V.scalar_tensor_tensor, ufp.t[:, ct, 0:T], Av[:, ct, :], prm.t[:, 16 + ct:17 + ct], sg.t[:, ct, 0:T],
                       ALU.add, ALU.mult, reads=[BA, prm.b, sg.b], writes=[ufp.b])
            elif name == 'zb':
                Av, BA = inproj_fm(T, 6)
                op('act', A_.activation, zbs.t[:, :, 0:T], Av, AF.Silu, reads=[BA], writes=[zbs.b])

        def stage_B(T, ntile, P, ufp, sg):
            for name in ['f', 'q', 'v', 'za', 'gate', 'a', 'zb']:
                grp(name, T, ntile, P)

        def hgrn_pre(T, C, roff, ntile, P, phases='ABCD'):
            NC_ = T // C
            qs, k0, lf, rel, egp, egn = W[0], W[1], W[2], W[3], W[4], W[5]
            ng1 = NGp.t[:, :, 1:T + 1].rearrange("p h (c j) -> p h c j", j=C)
            ng0 = NGp.t[:, :, 0:T].rearrange("p h (c j) -> p h c j", j=C)
            Rr = ng0[:, :, :, roff:roff + 1]
            Ll = ng1[:, :, :, C - 1:C]
            Aa = ng0[:, :, :, 0:1]
            rel4 = rel.t[:, :, 0:T].rearrange("p h (c j) -> p h c j", j=C)
            lf4 = lf.t[:, :, 0:T].rearrange("p h (c j) -> p h c j", j=C)
            if 'A' in phases:
                for h in range(4):
                    op('act', A_.activation, lf.t[:, h, 0:T], k0.t[:, h, 0:T], AF.Ln, scale=omlb.t[:, 4 + h:5 + h], bias=1.0,
                       reads=[k0.b, omlb.b], writes=[lf.b])
            if 'B' in phases:
                for h in range(4):
                    op('dve', V.tensor_tensor_scan, NGp.t[:, h, 1:T + 1], cst.t[:, 0:1].to_broadcast([128, T]),
                       lf.t[:, h, 0:T], 0.0, ALU.mult, ALU.subtract, reads=[lf.b, cst.b], writes=[NGp.b])
                op('dve', V.tensor_tensor, lf4, Ll.to_broadcast([128, 4, NC_, C]), ng1, ALU.subtract,
                   reads=[NGp.b], writes=[lf.b])
                op('dve', V.tensor_tensor, rel4, ng1, Rr.to_broadcast([128, 4, NC_, C]), ALU.subtract,
                   reads=[NGp.b], writes=[rel.b])
                ddv = [dd.t[:, i, :, 0:NC_] for i in range(3)]
                sq = lambda a: a.rearrange("p h c o -> p h (c o)")
                op('dve', V.tensor_tensor, ddv[0], sq(Rr), sq(Aa), ALU.subtract, reads=[NGp.b], writes=[dd.b])
                op('dve', V.tensor_tensor, ddv[1], sq(Ll), sq(Rr), ALU.subtract, reads=[NGp.b], writes=[dd.b])
                op('dve', V.tensor_tensor, ddv[2], sq(Ll), sq(Aa), ALU.subtract, reads=[NGp.b], writes=[dd.b])
            if 'C' in phases:
                op('act', A_.activation, lf.t[:, :, 0:T], lf.t[:, :, 0:T], AF.Exp, scale=-1.0, reads=[lf.b], writes=[lf.b])
                op('act', A_.activation, egp.t[:, :, 0:T], rel.t[:, :, 0:T], AF.Exp, scale=-1.0, reads=[rel.b], writes=[egp.b])
                op('act', A_.activation, egn.t[:, :, 0:T], rel.t[:, :, 0:T], AF.Exp, reads=[rel.b], writes=[egn.b])
                op('act', A_.activation, ee.t[:, :, :, 0:NC_], dd.t[:, :, :, 0:NC_], AF.Exp, scale=-1.0, reads=[dd.b], writes=[ee.b])
            if 'D' in phases:
                op('dve', V.tensor_tensor, qT.t[:, :, 0:T], qs.t[:, :, 0:T], egp.t[:, :, 0:T], ALU.mult,
                   reads=[qs.b, egp.b], writes=[qT.b])
                for h in range(4):
                    op('dve', V.scalar_tensor_tensor, kT.t[:, h, 0:T], k0.t[:, h, 0:T], omlb.t[:, h:h + 1], egn.t[:, h, 0:T],
                       ALU.mult, ALU.mult, reads=[k0.b, omlb.b, egn.b], writes=[kT.b])
                for h in range(4):
                    op('dve', V.scalar_tensor_tensor, keT.t[:, h, 0:T], k0.t[:, h, 0:T], omlb.t[:, h:h + 1], lf.t[:, h, 0:T],
                       ALU.mult, ALU.mult, reads=[k0.b, omlb.b, lf.b], writes=[keT.b])

        def hgrn_pre2(T, ntile, P):
            PPb0 = PP[:, 0:256].bitcast(BF16)
            PPb1 = PP[:, 256:512].bitcast(BF16)
            for i in range(ntile):
                tgt, tb = (PPb0, BPP) if i == 0 else (PPb1, BPP)
                for h in range(4):
                    op('pe', PE_.transpose, tgt[0:P, h * 128:(h + 1) * 128], keT.t[:, h, i * 128:i * 128 + P], idb.t[:, :],
                       reads=[keT.b, idb.b], writes=[tb])
            for i in range(ntile):
                tgt, tb = (PPb0, BPP) if i == 0 else (PPb1, BPP)
                op('act', A_.copy, ktok.t[0:P, i, :], tgt[0:P, :], reads=[tb], writes=[ktok.b])

        post_state = {}

        def post_a(T):
            OTv = OT[:, 0:4 * TB].rearrange("p (h t) -> p h t", h=4)[:, :, 0:T]
            op('act', A_.activation, sqa.t[:, :, 0:T], OTv, AF.Square, reads=[BOT], writes=[sqa.b])

        def post_b(T):
            rstd = W[3]
            Aps, BA = getA()
            Av = Aps[:, 0:4 * T].rearrange("p (j t) -> p j t", j=4)
            for j in range(4):
                op('pe', PE_.matmul, Av[:, j, :], on128.t[:, :], sqa.t[:, j, 0:T], start=True, stop=True,
                   reads=[on128.b, sqa.b], writes=[BA])
            op('act', A_.activation, rstd.t[:, :, 0:T], Av, AF.Ln, bias=cst.t[:, 2:3], reads=[BA, cst.b], writes=[rstd.b])
            op('act', A_.activation, rstd.t[:, :, 0:T], rstd.t[:, :, 0:T], AF.Exp, scale=-0.5, reads=[rstd.b], writes=[rstd.b])

        def post_c(T):
            OTv = OT[:, 0:4 * TB].rearrange("p (h t) -> p h t", h=4)[:, :, 0:T]
            rstd = W[3]
            oa = rstd
            op('dve', V.scalar_tensor_tensor, oa.t[:, :, 0:T], OTv, prm.t[:, 36:37], rstd.t[:, :, 0:T], ALU.mult, ALU.mult,
               reads=[BOT, prm.b, rstd.b], writes=[oa.b])
            op('dve', V.tensor_tensor, oT.t[:, 0:4, 0:T], oa.t[:, :, 0:T], zas.t[:, :, 0:T], ALU.mult,
               reads=[oa.b, zas.b], writes=[oT.b])

        def hgrn_post(T):
            post_a(T)
            post_b(T)
            post_c(T)

        def conv_mm(T):
            Aps, BA = getA()
            Av = Aps[:, 0:4 * T].rearrange("p (j t) -> p j t", j=4)
            for ct in range(4):
                for jg in range(8):
                    for q in range(4):
                        cg = 4 * ct + q
                        kw = {'tile_position': (0, 96)} if q == 3 else {}
                        op('pe', PE_.matmul, Av[32 * q:32 * q + 32, ct, :], WK[:, cg, jg, :], UBk.t[:, cg, 4 * jg:4 * jg + T],
                           start=(jg == 0), stop=(jg == 7), reads=[BWKa, BWKb, UBk.b], writes=[BA], **kw)
            return Av, BA

        def conv_dve_ops(T):
            acc = W[2]
            accb = [Buf("acc%d" % ct) for ct in range(4)]
            ops = []
            for jj in range(31):
                for ct in range(4):
                    av = acc.t[:, ct, 0:T].rearrange("p (s t) -> p s t", t=4)
                    wr = [accb[ct]] + ([acc.b] if jj == 0 else [])
                    if jj == 0:
                        ops.append(lambda av=av, ct=ct, wr=wr: op('dve', V.tensor_scalar, av, UBs.t[:, ct, :, 0:4], cwT.t[:, ct, 1:2], None,
                                                                  ALU.mult, reads=[UBs.b, cwT.b], writes=wr))
                    else:
                        ops.append(lambda av=av, ct=ct, jj=jj, wr=wr: op('dve', V.scalar_tensor_tensor, av, UBs.t[:, ct, :, jj:jj + 4],
                                                                         cwT.t[:, ct, 1 + jj:2 + jj], av, ALU.mult, ALU.add,
                                                                         reads=[UBs.b, cwT.b], writes=wr))
            return ops, acc.t[:, :, 0:T], [acc.b] + accb

        def ln1(T, Av, BA, mid=None, defer=False):
            cvb = W[3]
            for ct in range(4):
                op('act', A_.activation, cvb.t[:, ct, 0:T], Av[:, ct, :], AF.Identity, bias=prm.t[:, 24 + ct:25 + ct],
                   reads=[BA, prm.b], writes=[cvb.b])
                op('act', A_.activation, sqa.t[:, ct, 0:T], Av[:, ct, :], AF.Square, bias=prm.t[:, 24 + ct:25 + ct],
                   reads=[BA, prm.b], writes=[sqa.b])
            if mid is not None:
                mid()
            if defer:
                return None, None
            return ln1b(T)

        def ln1b(T):
            cvb = W[3]
            Aps2, BA2 = getA()
            for ct in range(4):
                op('pe', PE_.matmul, Aps2[:, 0:T], on512f.t[:, :], cvb.t[:, ct, 0:T], start=(ct == 0), stop=(ct == 3),
                   reads=[on512f.b, cvb.b], writes=[BA2])
            for ct in range(4):
                op('pe', PE_.matmul, Aps2[:, 512:512 + T], on512.t[:, :], sqa.t[:, ct, 0:T], start=(ct == 0), stop=(ct == 3),
                   reads=[on512.b, sqa.b], writes=[BA2])
            return Aps2, BA2

        def ln2(T, Aps2, BA2):
            mean_ps = Aps2[:, 0:T]
            msq_ps = Aps2[:, 512:512 + T]
            op('act', A_.activation, sml.t[:, 0:T], mean_ps, AF.Square, reads=[BA2], writes=[sml.b])
            op('act', A_.copy, mean_sb.t[:, 0:T], mean_ps, reads=[BA2], writes=[mean_sb.b])
            op('dve', V.tensor_tensor, sml.t[:, 0:T], msq_ps, sml.t[:, 0:T], ALU.subtract, reads=[BA2, sml.b], writes=[sml.b])
            op('act', A_.activation, sml.t[:, 0:T], sml.t[:, 0:T], AF.Ln, bias=cst.t[:, 2:3], reads=[sml.b, cst.b], writes=[sml.b])
            op('act', A_.activation, rsl.t[:, 0:T], sml.t[:, 0:T], AF.Exp, scale=-0.5, reads=[sml.b], writes=[rsl.b])

        def ln3(T, Aps2, BA2):
            cvb = W[3]
            op('dve', V.tensor_tensor, cvb.t[:, :, 0:T], cvb.t[:, :, 0:T],
               mean_sb.t[:, 0:T].rearrange("p (o t) -> p o t", o=1).to_broadcast([128, 4, T]), ALU.subtract,
               reads=[mean_sb.b, cvb.b], writes=[cvb.b])
            op('dve', V.tensor_tensor, cvb.t[:, :, 0:T], cvb.t[:, :, 0:T],
               rsl.t[:, 0:T].rearrange("p (o t) -> p o t", o=1).to_broadcast([128, 4, T]), ALU.mult,
               reads=[rsl.b, cvb.b], writes=[cvb.b])

        def ln4(T):
            cvb = W[3]
            for ct in range(4):
                op('act', A_.activation, cvy.t[:, ct, 0:T], cvb.t[:, ct, 0:T], AF.Silu, scale=prm.t[:, 28 + ct:29 + ct],
                   bias=prm.t[:, 32 + ct:33 + ct], reads=[cvb.b, prm.b], writes=[cvy.b])
            op('dve', V.tensor_tensor, oT.t[:, 4:8, 0:T], cvy.t[:, :, 0:T], zbs.t[:, :, 0:T], ALU.mult,
               reads=[cvy.b, zbs.b], writes=[oT.b])

        def conv_ln_sample(T, Av, BA):
            Aps2, BA2 = ln1(T, Av, BA)
            ln2(T, Aps2, BA2)
            ln3(T, Aps2, BA2)
            ln4(T)

        e_state = {}

        def stage_E(tiles, dsts, phases='01234'):
            if '0' in phases:
                slots_e = []
                for i, (xap, bx, P) in enumerate(tiles):
                    if i == 0:
                        Aps, BA = getA()
                    else:
                        Aps, BA = OT, BOT
                    for hh in range(2):
                        for k in range(8):
                            op('pe', PE_.matmul, Aps[0:P, hh * 512:(hh + 1) * 512], oT.t[:, k, i * 128:i * 128 + P],
                               w_out_bf[:, k, hh * 512:(hh + 1) * 512], start=(k == 0), stop=(k == 7),
                               reads=[oT.b, Bwout], writes=[BA])
                    slots_e.append((Aps, BA))
                e_state['slots'] = slots_e
            slots_e = e_state['slots']
            if '1' in phases:
                for i, (xap, bx, P) in enumerate(tiles):
                    Aps, BA = slots_e[i]
                    op('dve', V.tensor_tensor, xap, Aps[0:P, :], xap, ALU.add, reads=[BA, bx], writes=[bx])
            if '2' in phases:
                for i, (xap, bx, P) in enumerate(tiles):
                    op('act', A_.activation, sqa.t[0:P, :, :].rearrange("p a b -> p (a b)"), xap, AF.Square,
                       accum_out=stt_.t[0:P, 12 + i:13 + i], reads=[bx], writes=[sqa.b, BstE])
            if '3' in phases:
                for i, (xap, bx, P) in enumerate(tiles):
                    op('pool', G.tensor_scalar, stt_.t[0:P, 12 + i:13 + i], stt_.t[0:P, 12 + i:13 + i], 1.0 / D, EPS, ALU.mult, ALU.add,
                       reads=[BstE], writes=[BstE])
                    op('pool', G.tensor_tensor, stt_.t[0:P, 14 + i:15 + i], stt_.t[0:P, 12 + i:13 + i], cst.t[0:P, 1:2], ALU.pow,
                       reads=[BstE, cst.b], writes=[BstE])
            if '4' in phases:
                for i, (xap, bx, P) in enumerate(tiles):
                    op('dve', V.scalar_tensor_tensor, xap, xap, stt_.t[0:P, 14 + i:15 + i], gfin.t[0:P, :], ALU.mult, ALU.mult,
                       reads=[bx, BstE, gfin.b], writes=[bx])
                    fw.dma('sp', dsts[i], xap, bx, reads=[bx])
                    if bx not in out_bufs:
                        out_bufs.append(bx)

        with ExitStack() as es_s:
            S0 = [TT("S0_%d" % i, [128, 4, 128], F32) for i in range(2)]
            S0bf = [TT("S0bf_%d" % i, [128, 4, 128], BF16) for i in range(2)]
            UBs = TT("UBs", [128, 4, NSS, 34], BF16)
            kem = [TT("kem_%d" % i, [64, 512], BF16) for i in range(2)]
            UB = UBs
            T = TS
            for g4 in range(4):
                fw.dma('sp', XT[0:120, 1 + g4 // 2, (g4 % 2) * 512:(g4 % 2 + 1) * 512],
                       st_c[g4 * 4:(g4 + 1) * 4, :, :].rearrange("s j c -> (s j) c"), Bx[1 + g4 // 2], writes=[Bx[1 + g4 // 2]])
            Bd2d = Buf("d2d")
            fw.dma('sp', sc_s[:, 0:26, :], st_c[:, 4:30, :], Bd2d)
            out_bufs.append(Bd2d)
            for ct in range(4):
                Aps, BA = getA()
                for g4 in range(4):
                    op('pe', PE_.transpose, Aps[:, g4 * 120:(g4 + 1) * 120],
                       XT[0:120, 1 + g4 // 2, (g4 % 2) * 512 + ct * 128:(g4 % 2) * 512 + (ct + 1) * 128],
                       idf.t[0:120, 0:120], reads=[Bx[1 + g4 // 2], idf.b], writes=[BA])
                op('act', A_.copy, UBs.t[:, ct, :, 0:30], Aps[:, 0:480].rearrange("p (s j) -> p s j", j=30),
                   reads=[BA], writes=[UBs.b])

            ck('s_hist')
            tiles = [(XT[0:TS, 0, :], Bx[0], TS)]
            stage_A(T, tiles)
            ck('s_A')
            ufp, sg = W[4], W[5]
            stage_B(T, 1, TS, ufp, sg)
            ck('s_B')
            for ct in range(4):
                op('pool', G.tensor_copy, UBs.t[:, ct, :, 30:34], ufp.t[:, ct, 0:T].rearrange("p (s t) -> p s t", t=4),
                   reads=[ufp.b], writes=[UBs.b])
            Aps, BA = getA()
            for ct in range(4):
                op('pe', PE_.transpose, Aps[0:TS, ct * 128:(ct + 1) * 128], ufp.t[:, ct, 0:T], idf.t[:, :],
                   reads=[ufp.b, idf.b], writes=[BA])
            op('act', A_.copy, utok.t[0:TS, :], Aps[0:TS, 0:512], reads=[BA], writes=[utok.b])
            for s in range(NSS):
                fw.dma('sp', sc_s[s, 26:30, :], utok.t[s * 4:(s + 1) * 4, :], utok.b, reads=[utok.b])
            out_bufs.append(utok.b)

            ck('s_u')
            hgrn_pre(T, 4, 0, 1, TS)
            hgrn_pre2(T, 1, TS)
            cv_ops, cvAv, cvBA = conv_dve_ops(T)
            ck('s_pre')
            for h in range(4):
                op('pe', PE_.matmul, SC[0:TS, h * 64:(h + 1) * 64], kT.t[:, h, 0:T], qT.t[:, h, 0:T], start=True, stop=True,
                   reads=[kT.b, qT.b], writes=[BSC])
            ck('h0')
            op('dve', V.scalar_tensor_tensor, ssb.t[0:TS, 0:256], SC[0:TS, :], BIG, masks.t[0:TS, :], ALU.min, ALU.mult,
               reads=[BSC, masks.b], writes=[ssb.b])
            ck('h1')
            for h in range(4):
                op('pe', PE_.matmul, OT[:, h * TB:h * TB + T], vtok.t[0:TS, 0, h * 128:(h + 1) * 128],
                   ssb.t[0:TS, h * 64:(h + 1) * 64], start=(h % 2 == 0), stop=False, skip_group_check=True,
                   reads=[vtok.b, ssb.b], writes=[BOT])
            ck('h2')
            BtS = [Buf("tS0"), Buf("tS1")]
            fw.dma('sp', S0[0].t[:], st_h[0].rearrange("h d v -> d h v"), S0[0].b, writes=[S0[0].b])
            for s in range(NSS):
                sl = s % 2
                for _ in range(8):
                    if cv_ops:
                        cv_ops.pop(0)()
                if s + 1 < NSS:
                    fw.dma('sp', S0[1 - sl].t[:], st_h[s + 1].rearrange("h d v -> d h v"), S0[1 - sl].b, writes=[S0[1 - sl].b])
                op('act', A_.copy, S0bf[sl].t[:], S0[sl].t[:], reads=[S0[sl].b], writes=[S0bf[sl].b])
                if s == 0: ck('h3')
                for h in range(4):
                    op('pe', PE_.matmul, OT[:, h * TB + s * 4:h * TB + s * 4 + 4], S0bf[sl].t[:, h, :],
                       qT.t[:, h, s * 4:s * 4 + 4], start=False, stop=(s == NSS - 1), skip_group_check=True,
                       reads=[S0bf[sl].b, qT.b], writes=[BOT])
                if s == 0: ck('h4')
                if s == 0:
                    op('pool', G.tensor_scalar, kem[0].t[:, :], ktok.t[0:TS, 0, :], mcol.t[:, 0:1], 1.0, ALU.mult, ALU.mult,
                       reads=[ktok.b, mcol.b], writes=[kem[0].b])
                if s + 1 < NSS:
                    op('pool', G.tensor_scalar, kem[1 - sl].t[:, :], ktok.t[0:TS, 0, :], mcol.t[:, s + 1:s + 2], 1.0, ALU.mult, ALU.mult,
                       reads=[ktok.b, mcol.b], writes=[kem[1 - sl].b])
                if s == 0: ck('h5')
                ppap, ppb = (PP, BPP) if sl == 0 else (MISC, BSC)
                for h in range(4):
                    op('pe', PE_.matmul, ppap[:, h * 128:(h + 1) * 128], kem[sl].t[:, h * 128:(h + 1) * 128],
                       vtok.t[0:TS, 0, h * 128:(h + 1) * 128], start=True, stop=True,
                       reads=[kem[sl].b, vtok.b], writes=[ppb])
                if s == 0: ck('h6')
                tSl = XT[:, 3, sl * 512:(sl + 1) * 512].rearrange("p (h v) -> p h v", h=4)
                wr = [BtS[sl]] + ([Bx[3]] if s < 2 else [])
                for h in range(4):
                    op('dve', V.scalar_tensor_tensor, tSl[:, h, :], S0[sl].t[:, h, :], ee.t[:, 2, h, s:s + 1],
                       ppap[:, h * 128:(h + 1) * 128], ALU.mult, ALU.add, reads=[S0[sl].b, ee.b, ppb], writes=wr)
                if s == 0: ck('h8')
                fw.dma('pool', sh_s[s].rearrange("h d v -> d h v"), tSl, BtS[sl], reads=[BtS[sl]])
            ck('s_hgrn')
            while cv_ops:
                cv_ops.pop(0)()
            conv_ln_sample(T, cvAv, cvBA)
            hgrn_post(T)
            ck('s_post')
            stage_E(tiles, [y_s[:, :]])
            ck('s_E')

        with ExitStack() as es_p:
            S = S0[0]
            Sbf2 = [S0bf[0], S0bf[1]]
            T = TB
            op('pool', G.memset, S.t[:], 0.0, writes=[S.b])
            op('pool', G.memset, Ue.t[:, :, 0:31], 0.0, writes=[Ue.b])

            def xtiles(b):
                base = ((b + 1) % 3) * 2
                return [(XT[:, base + i, :], Bx[base + i], 128) for i in range(2)]

            def xload(b):
                for i, (xap, bx, P) in enumerate(xtiles(b)):
                    extra = [BtS[0], BtS[1]] if (b == 0 and i == 1) else []
                    fw.dma('sp', xap, x_p[b * TB + i * 128:b * TB + (i + 1) * 128, :], bx, writes=[bx] + extra)

            def after_a(bb):
                ufp = W[4]
                if bb > 0:
                    op('pool', G.tensor_copy, Ue.t[:, :, 0:31], Ue.t[:, :, T:T + 31], reads=[Ue.b], writes=[Ue.b])
                op('pool', G.tensor_copy, Ue.t[:, :, 31:31 + T], ufp.t[:, :, 0:T], reads=[ufp.b], writes=[Ue.b])
                for j in range(4):
                    for r in range(4):
                        fw.dma('sp', UBk.t[32 * j:32 * j + 32, r:16:4, :], Ue.t[32 * r:32 * r + 32, :, j:j + T + 28],
                               UBk.b, reads=[Ue.b], writes=[UBk.b], nowaw=True)
                if bb == NBLK - 1:
                    Aps, BA = getA()
                    for ct in range(4):
                        op('pe', PE_.transpose, Aps[0:30, ct * 128:(ct + 1) * 128], ufp.t[:, ct, T - 30:T], idf.t[:, :],
                           reads=[ufp.b, idf.b], writes=[BA])
                    op('act', A_.copy, utok.t[0:30, :], Aps[0:30, 0:512], reads=[BA], writes=[utok.b])
                    fw.dma('sp', sc_p[:, :], utok.t[0:30, :], utok.b, reads=[utok.b])

            xload(0)
            if NBLK > 1:
                xload(1)
            stage_A(T, xtiles(0))
            for name in ['f', 'q', 'gate', 'a']:
                grp(name, T, 2, 128)
            after_a(0)
            grp('zb', T, 2, 128)
            for b in range(NBLK):
                nxt = b + 1 < NBLK
                tiles = xtiles(b)
                if b == 0:
                    if b + 2 < NBLK:
                        xload(b + 2)
                    hgrn_pre(T, 64, 32, 2, 128, 'A')
                    if nxt:
                        stage_A1(T, xtiles(b + 1))
                    hgrn_pre(T, 64, 32, 2, 128, 'B')
                    grp('za', T, 2, 128)
                    grp('v', T, 2, 128)
                    hgrn_pre(T, 64, 32, 2, 128, 'CD')
                else:
                    prev_tiles, prev_dsts = pending_E
                    hgrn_pre(T, 64, 32, 2, 128, 'A')
                    hgrn_pre(T, 64, 32, 2, 128, 'B')
                    hgrn_pre(T, 64, 32, 2, 128, 'C')
                    stage_E(prev_tiles, prev_dsts, '1')
                    stage_E(prev_tiles, prev_dsts, '23')
                    hgrn_pre(T, 64, 32, 2, 128, 'D')
                    stage_E(prev_tiles, prev_dsts, '4')
                    if b + 2 < NBLK:
                        xload(b + 2)
                Avc, BAc = conv_mm(T)
                ln1(T, Avc, BAc, mid=(lambda: stage_A2(T, xtiles(b + 1))) if nxt else None, defer=True)
                NCH = T // 64
                SCF = MISC[:, 0:512]
                for c in range(NCH):
                    po = (c % 2) * 64
                    cs = slice(c * 64, (c + 1) * 64)
                    for h in range(4):
                        op('pe', PE_.matmul, SCF[po:po + 64, (c // 2) * 256 + h * 64:(c // 2) * 256 + (h + 1) * 64],
                           kT.t[:, h, cs], qT.t[:, h, cs], start=True, stop=True, reads=[kT.b, qT.b], writes=[BSC])
                for o in range(2):
                    op('dve', V.scalar_tensor_tensor, ssb.t[:, o * 256:(o + 1) * 256], SCF[:, o * 256:(o + 1) * 256], BIG,
                       maskp.t[:, :], ALU.min, ALU.mult, reads=[BSC, maskp.b], writes=[ssb.b])
                hgrn_pre2(T, 2, 128)
                pp_of = [(PP[:, :], BPP), (MISC[:, :], BSC), (PP[:, :], BPP), (MISC[:, :], BSC)]

                def emit_P(c):
                    i = c // 2
                    po = (c % 2) * 64
                    pap, pb = pp_of[c]
                    for h in range(4):
                        op('pe', PE_.matmul, pap[:, h * 128:(h + 1) * 128], ktok.t[po:po + 64, i, h * 128:(h + 1) * 128],
                           vtok.t[po:po + 64, i, h * 128:(h + 1) * 128], start=True, stop=True,
                           reads=[ktok.b, vtok.b], writes=[pb])
                Aps2, BA2 = ln1b(T)
                ln2(T, Aps2, BA2)
                emit_P(0)
                emit_P(1)
                if nxt:
                    grp('f', T, 2, 128)
                    grp('q', T, 2, 128)
                for c in range(NCH):
                    i = c // 2
                    po = (c % 2) * 64
                    cs = slice(c * 64, (c + 1) * 64)
                    pap, pb = pp_of[c]
                    Sbf = Sbf2[c % 2]
                    op('dve', V.tensor_tensor, Sbf.t[:], S.t[:], ee.t[:, 0, :, c:c + 1].to_broadcast([128, 4, 128]), ALU.mult,
                       reads=[S.b, ee.b], writes=[Sbf.b])
                    for h in range(4):
                        op('dve', V.scalar_tensor_tensor, S.t[:, h, :], S.t[:, h, :], ee.t[:, 2, h, c:c + 1],
                           pap[:, h * 128:(h + 1) * 128], ALU.mult, ALU.add, reads=[ee.b, pb], writes=[S.b])
                    if c + 2 < NCH:
                        emit_P(c + 2)
                    for h in range(4):
                        oap = OT[:, h * TB + c * 64:h * TB + (c + 1) * 64]
                        op('pe', PE_.matmul, oap, vtok.t[po:po + 64, i, h * 128:(h + 1) * 128],
                           ssb.t[po:po + 64, (c // 2) * 256 + h * 64:(c // 2) * 256 + (h + 1) * 64], start=True, stop=False,
                           reads=[vtok.b, ssb.b], writes=[BOT])
                        op('pe', PE_.matmul, oap, Sbf.t[:, h, :], qT.t[:, h, cs], start=False, stop=True,
                           reads=[Sbf.b, qT.b], writes=[BOT])
                    if c == 1:
                        ln3(T, Aps2, BA2)
                        if b + 2 < NBLK:
                            stage_A1(T, xtiles(b + 2))
                post_a(T)
                if nxt:
                    grp('gate', T, 2, 128)
                    grp('a', T, 2, 128)
                    after_a(b + 1)
                ln4(T)
                post_b(T)
                post_c(T)
                if nxt:
                    grp('zb', T, 2, 128)
                    grp('za', T, 2, 128)
                    grp('v', T, 2, 128)
                dsts_b = [y_p[b * TB + i * 128:b * TB + (i + 1) * 128, :] for i in range(2)]
                if nxt:
                    stage_E(tiles, dsts_b, '0')
                    pending_E = (tiles, dsts_b)
                else:
                    stage_E(tiles, dsts_b)
            fw.dma('sp', sh_p.rearrange("h d v -> d h v"), S.t[:], S.b, reads=[S.b])
            out_bufs.append(S.b)
            fw.wait_all('sp', out_bufs)
            fw.barrier()


_NC_CACHE = {}


def _consts():
    idf = np.eye(128, dtype=np.float32)
    idb = idf.astype(ml_dtypes.bfloat16)
    s = np.arange(64)[:, None]
    t = np.arange(64)[None, :]
    mp = (s <= t).astype(np.float32)
    maskp = np.tile(np.tile(mp, (1, 4)), (2, 1))
    ms = ((s // 4 == t // 4) & (s <= t)).astype(np.float32)
    masks = np.tile(np.tile(ms, (1, 4)), (2, 1))
    mcol = (np.arange(64)[:, None] // 4 == np.arange(16)[None, :]).astype(np.float32)
    return idb, idf, np.ascontiguousarray(maskp), np.ascontiguousarray(masks), np.ascontiguousarray(mcol)


def kernel(x_prompt, x_sample, state_hgrn, state_conv, norm_in_g, w_in, lb_logits, hgrn_norm_g,
           b_glu, conv_w, conv_b, ln_g, ln_b, w_out, final_norm_g):
    f = lambda a: np.ascontiguousarray(np.asarray(a, dtype=np.float32))
    if 'nc' not in _NC_CACHE:
        _NC_CACHE['nc'] = build_nc()
    nc = _NC_CACHE['nc']
    idb, idf, maskp, masks, mcol = _consts()
    x_prompt = f(x_prompt); x_sample = f(x_sample); state_hgrn = f(state_hgrn); state_conv = f(state_conv)
    shared = {
        "norm_in_g": f(norm_in_g).reshape(8, 128), "w_in": f(w_in).reshape(D, INW),
        "lb_logits": f(lb_logits).reshape(8, 128), "hgrn_norm_g": f(hgrn_norm_g).reshape(1, 128),
        "b_glu": f(b_glu).reshape(8, 128), "conv_w": f(conv_w).reshape(31, 512),
        "conv_b": f(conv_b).reshape(4, 128), "ln_g": f(ln_g).reshape(4, 128), "ln_b": f(ln_b).reshape(4, 128),
        "w_out": f(w_out).reshape(D, D), "final_g": f(final_norm_g).reshape(1, D),
        "c_idb": idb, "c_idf": idf, "c_maskp": maskp, "c_masks": masks, "c_mcol": mcol,
    }
    in_maps = []
    for c in range(NCORES):
        m = dict(shared)
        m["x_p"] = np.ascontiguousarray(x_prompt[c])
        m["x_s"] = np.ascontiguousarray(x_sample[c * NSS:(c + 1) * NSS].reshape(TS, D))
        m["st_h"] = np.ascontiguousarray(state_hgrn[0, c * NSS:(c + 1) * NSS])
        m["st_c"] = np.ascontiguousarray(state_conv[0, c * NSS:(c + 1) * NSS])
        in_maps.append(m)
    res = run_bass_kernel_spmd(nc, in_maps, core_ids=list(range(NCORES)))
    R = res.results
    y_prompt = np.stack([R[c]["y_p"] for c in range(NCORES)], 0)
    y_sample = np.concatenate([R[c]["y_s"].reshape(NSS, 4, D) for c in range(NCORES)], 0)
    shp = np.stack([R[c]["sh_p"] for c in range(NCORES)], 0)[None]
    scp = np.stack([R[c]["sc_p"] for c in range(NCORES)], 0)[None]
    shs = np.concatenate([R[c]["sh_s"] for c in range(NCORES)], 0)[None]
    scs = np.concatenate([R[c]["sc_s"] for c in range(NCORES)], 0)[None]
    return (y_prompt.astype(np.float32), y_sample.astype(np.float32), shp.astype(np.float32),
            scp.astype(np.float32), shs.astype(np.float32), scs.astype(np.float32))
```

```python
import numpy as np
from contextlib import ExitStack
import ml_dtypes
import concourse.bass as bass
import concourse.mybir as mybir
from concourse.bass_utils import run_bass_kernel_spmd

F32 = mybir.dt.float32
BF16 = mybir.dt.bfloat16
ALU = mybir.AluOpType
AF = mybir.ActivationFunctionType

NCORES = 8
D = 1024
SEQ = 2048
TB = 256
NBLK = SEQ // TB
NSS = 16
TS = NSS * 4
INW = 3584
EPS = 1e-6
BIG = 3.0e38
SAME_ENG_SYNC = ('act', 'dve', 'pool')


class Buf:
    def __init__(self, name, excl=False):
        self.name = name
        self.excl = excl
        self.w = {}
        self.r = {}
        self.dsem = None
        self.dcnt = 0


class FW:
    def __init__(self, nc, es):
        self.nc = nc
        self.es = es
        self.engs = {'pe': nc.tensor, 'act': nc.scalar, 'dve': nc.vector,
                     'pool': nc.gpsimd, 'sp': nc.sync}
        self.cnt = {}
        self.seen = {}
        self.semobj = {}
        self.dcount = {}
        for k in self.engs:
            self.semobj[k] = es.enter_context(nc.semaphore('s_' + k))
            self.cnt[k] = 0
            self.seen[k] = {}
        self.ndsem = 0

    def sb(self, name, shape, dt=F32, es=None):
        return (es or self.es).enter_context(self.nc.sbuf_tensor(name, list(shape), dt))

    def ps(self, name, shape, dt=F32):
        return self.es.enter_context(self.nc.psum_tensor(name, list(shape), dt))

    def _wait(self, eng, key, val):
        if key == eng and eng not in SAME_ENG_SYNC:
            return
        if self.seen[eng].get(key, 0) >= val:
            return
        self.engs[eng].wait_ge(self.semobj[key], val)
        self.seen[eng][key] = val

    def _deps(self, eng, reads, writes):
        need = {}
        for b in reads:
            for k, v in b.w.items():
                need[k] = max(need.get(k, 0), v)
        for b in writes:
            for k, v in b.w.items():
                if k == eng:
                    continue
                need[k] = max(need.get(k, 0), v)
            for k, v in b.r.items():
                need[k] = max(need.get(k, 0), v)
        for k, v in need.items():
            self._wait(eng, k, v)

    def _commit(self, key, val, reads, writes):
        for b in writes:
            if b.r:
                b.w = {}
                b.r = {}
            b.w[key] = max(b.w.get(key, 0), val)
        for b in reads:
            b.r[key] = max(b.r.get(key, 0), val)

    def op(self, eng, fn, *args, reads=(), writes=(), **kw):
        def _flat(xs):
            out = []
            for x in xs:
                if isinstance(x, (list, tuple)):
                    out.extend(_flat(x))
                else:
                    out.append(x)
            return out
        reads = _flat(reads)
        writes = _flat(writes)
        writes = list(writes) + [b for b in reads if b.excl and b not in writes]
        reads = [b for b in reads if not b.excl]
        self._deps(eng, reads, writes)
        ins = fn(*args, **kw)
        if eng == 'pe' and kw.get('stop') is False:
            self._commit(eng, self.cnt[eng] + 1, reads, writes)
            return ins
        self.cnt[eng] += 1
        ins.then_inc(self.semobj[eng], 1)
        self._commit(eng, self.cnt[eng], reads, writes)
        return ins

    def dma(self, eng, out, in_, tag, reads=(), writes=(), nowaw=False, **kw):
        if nowaw and tag.dsem is not None:
            saved = [(b, b.w.pop(tag.dsem, None)) for b in writes]
            self._deps(eng, reads, writes)
            for b, v in saved:
                if v is not None:
                    b.w[tag.dsem] = v
        else:
            self._deps(eng, reads, writes)
        if tag.dsem is None:
            tag.dsem = 'd%d' % self.ndsem
            self.ndsem += 1
            self.semobj[tag.dsem] = self.es.enter_context(self.nc.semaphore(tag.dsem))
        ins = self.engs[eng].dma_start(out=out, in_=in_, **kw)
        tag.dcnt += 1
        ins.then_inc(self.semobj[tag.dsem], 16)
        self.dcount[tag.dsem] = tag.dcnt * 16
        self._commit(tag.dsem, tag.dcnt * 16, reads, writes)
        return ins

    def wait_all(self, eng, bufs):
        self._deps(eng, bufs, bufs)

    def barrier(self):
        for e in self.engs:
            for k in self.engs:
                if k != e and self.cnt[k] > 0:
                    self._wait(e, k, self.cnt[k])
            for k, v in self.dcount.items():
                self._wait(e, k, v)


class _Stop(Exception):
    pass


DEBUG_STOP = None
DG_ENG = None


def build_nc():
    nc = bass.Bass("TRN2", target_bir_lowering=False)
    try:
        _build(nc)
    except _Stop:
        pass
    return nc


def _build(nc):

    def din(name, shape, dt=F32):
        return nc.dram_tensor(name, list(shape), dt, kind="ExternalInput").ap()

    def dout(name, shape, dt=F32):
        return nc.dram_tensor(name, list(shape), dt, kind="ExternalOutput").ap()

    x_p = din("x_p", [SEQ, D])
    x_s = din("x_s", [TS, D])
    st_h = din("st_h", [NSS, 4, 128, 128])
    st_c = din("st_c", [NSS, 30, 512])
    norm_in_g = din("norm_in_g", [8, 128])
    w_in = din("w_in", [D, INW])
    lb_logits = din("lb_logits", [8, 128])
    hgrn_norm_g = din("hgrn_norm_g", [1, 128])
    b_glu = din("b_glu", [8, 128])
    conv_w = din("conv_w", [31, 512])
    conv_b = din("conv_b", [4, 128])
    ln_g = din("ln_g", [4, 128])
    ln_b = din("ln_b", [4, 128])
    w_out = din("w_out", [D, D])
    final_g = din("final_g", [1, D])
    c_idb = din("c_idb", [128, 128], BF16)
    c_idf = din("c_idf", [128, 128])
    c_maskp = din("c_maskp", [128, 256])
    c_masks = din("c_masks", [128, 256])
    c_mcol = din("c_mcol", [64, 16])

    y_p = dout("y_p", [SEQ, D])
    y_s = dout("y_s", [TS, D])
    sh_p = dout("sh_p", [4, 128, 128])
    sc_p = dout("sc_p", [30, 512])
    sh_s = dout("sh_s", [NSS, 4, 128, 128])
    sc_s = dout("sc_s", [NSS, 30, 512])

    with ExitStack() as es:
        fw = FW(nc, es)
        op = fw.op
        V, A_, G, PE_ = nc.vector, nc.scalar, nc.gpsimd, nc.tensor
        out_bufs = []

        def ck(name):
            if DEBUG_STOP == name:
                fw.barrier()
                raise _Stop()

        class TT:
            def __init__(self, name, shape, dt=F32, scope=None):
                self.t = fw.sb(name, shape, dt, es=scope)
                self.b = Buf(name)

        A0 = fw.ps("A0", [128, 1024]); A1 = fw.ps("A1", [128, 1024])
        OT = fw.ps("OT", [128, 1024]); BOT = Buf("OT", True)
        PP = fw.ps("PP", [128, 512]); BPP = Buf("PP", True)
        MISC = fw.ps("MISC", [128, 512]); BSC = Buf("SC", True); BTRB = BSC
        SC = MISC[:, 0:256]
        TRB = MISC[:, 256:512].bitcast(BF16)
        slots = [(A0, Buf("A0", True)), (A1, Buf("A1", True))]
        ia = [0]

        def getA():
            s = slots[ia[0] % 2]
            ia[0] += 1
            return s

        w_in_bf = fw.sb("w_in_bf", [128, 8, INW], BF16)
        Bwin = [Buf("win%d" % g) for g in range(7)]
        w_out_bf = fw.sb("w_out_bf", [128, 8, D], BF16); Bwout = Buf("wout")
        WK = fw.sb("WK", [128, 16, 8, 32], BF16); BWKa = Buf("WKa"); BWKb = Buf("WKb")
        wsc = TT("wsc", [128, 4, 4, 8])
        cwTb = TT("cwTb", [128, 4, 32], BF16)
        idrep = TT("idrep", [128, 32], BF16)
        Ue = TT("Ue", [128, 4, 31 + TB], BF16)
        UBk = TT("UBk", [128, 16, TB + 28], BF16)
        gfin = TT("gfin", [128, D])
        idb = TT("idb", [128, 128], BF16)
        idf = TT("idf", [128, 128])
        maskp = TT("maskp", [128, 256])
        masks = TT("masks", [128, 256])
        mcol = TT("mcol", [64, 16])
        prm = TT("prm", [128, 40])
        cwT = TT("cwT", [128, 4, 32])
        on128 = TT("on128", [128, 128], BF16)
        on512 = TT("on512", [128, 128], BF16)
        on512f = TT("on512f", [128, 128])
        cst = TT("cst", [128, 8])
        omlb = TT("omlb", [128, 8])

        XT = fw.sb("XT", [128, 6, D]); Bx = [Buf("x%d" % i) for i in range(6)]
        xs = TT("xs", [128, 2, D], BF16)
        hT = TT("hT", [128, 8, TB], BF16)
        W = [TT("W%d" % i, [128, 4, TB]) for i in range(6)]
        NGp = TT("NGp", [128, 4, TB + 1])
        qT = TT("qT", [128, 4, TB], BF16)
        kT = TT("kT", [128, 4, TB], BF16)
        keT = TT("keT", [128, 4, TB], BF16)
        vtok = TT("vtok", [128, 2, 512], BF16)
        ktok = TT("ktok", [128, 2, 512], BF16)
        zas = TT("zas", [128, 4, TB], BF16)
        zbs = TT("zbs", [128, 4, TB], BF16)
        dd = TT("dd", [128, 3, 4, 16])
        ee = TT("ee", [128, 3, 4, 16])
        ssb = TT("ssb", [128, 512], BF16)
        sqa = TT("sqa", [128, 4, TB], BF16)
        cvy = TT("cvy", [128, 4, TB], BF16)
        sml = TT("sml", [128, TB])
        rsl = TT("rsl", [128, TB])
        mean_sb = TT("mean_sb", [128, TB])
        oT = TT("oT", [128, 8, TB], BF16)
        utok = TT("utok", [64, 512])
        stt_ = TT("stt_", [128, 16])
        prow = TT("prow", [40, 128])
        onerow = TT("onerow", [1, 128])

        def load(tt, src, eng='sp'):
            fw.dma(eng, tt.t[:], src, tt.b, writes=[tt.b])

        load(idb, c_idb[:, :]); load(idf, c_idf[:, :])
        rows = [(lb_logits, 0, 8), (norm_in_g, 8, 8), (b_glu, 16, 8), (conv_b, 24, 4),
                (ln_g, 28, 4), (ln_b, 32, 4), (hgrn_norm_g, 36, 1)]
        op('pool', G.memset, prow.t[:], 0.0, writes=[prow.b])
        for (src, r0, n) in rows:
            fw.dma('sp', prow.t[r0:r0 + n, :], src[:, :], prow.b, writes=[prow.b])
        class _V:
            pass
        cwrow = _V(); cwrow.t = XT[0:31, 2, 0:512]; cwrow.b = Bx[2]
        gfrow = _V(); gfrow.t = XT[0:1, 3, :]; gfrow.b = Bx[3]
        fw.dma('sp', cwrow.t, conv_w[:, :], cwrow.b, writes=[cwrow.b])
        fw.dma('sp', gfrow.t, final_g[:, :], gfrow.b, writes=[gfrow.b])
        load(maskp, c_maskp[:, :]); load(masks, c_masks[:, :]); load(mcol, c_mcol[:, :])
        op('pool', G.memset, cst.t[:, 0:1], 1.0, writes=[cst.b])
        op('pool', G.memset, cst.t[:, 1:2], -0.5, writes=[cst.b])
        op('pool', G.memset, cst.t[:, 2:3], EPS, writes=[cst.b])
        op('pool', G.memset, on128.t[:], 1.0 / 128, writes=[on128.b])
        op('pool', G.memset, on512.t[:], 1.0 / 512, writes=[on512.b])
        op('pool', G.memset, on512f.t[:], 1.0 / 512, writes=[on512f.b])
        op('pool', G.memset, onerow.t[:], 1.0, writes=[onerow.b])
        op('pool', G.memset, NGp.t[:, :, 0:1], 0.0, writes=[NGp.b])

        fw.dma('sp', XT[0:TS, 0, :], x_s[:, :], Bx[0], writes=[Bx[0]])

        ck('su0')
        Aps, BA = getA()
        op('pe', PE_.transpose, Aps[:, 0:40], prow.t[:, :], idf.t[0:40, 0:40], reads=[prow.b, idf.b], writes=[BA])
        op('act', A_.copy, prm.t[:], Aps[:, 0:40], reads=[BA], writes=[prm.b])
        Aps, BA = getA()
        for ct in range(4):
            op('pe', PE_.transpose, Aps[:, ct * 32:ct * 32 + 31], XT[0:31, 2, ct * 128:(ct + 1) * 128],
               idf.t[0:31, 0:31], reads=[cwrow.b, idf.b], writes=[BA])
        op('pool', G.memset, cwT.t[:, :, 0:1], 0.0, writes=[cwT.b])
        op('act', A_.copy, cwT.t[:, :, 1:32], Aps[:, 0:128].rearrange("p (c j) -> p c j", j=32)[:, :, 0:31],
           reads=[BA], writes=[cwT.b])
        op('act', A_.copy, cwTb.t[:], cwT.t[:], reads=[cwT.b], writes=[cwTb.b])
        op('dve', V.tensor_tensor, idrep.t[:], idb.t[:, 0:32], idb.t[:, 32:64], ALU.add, reads=[idb.b], writes=[idrep.b])
        op('dve', V.tensor_tensor, idrep.t[:], idrep.t[:], idb.t[:, 64:96], ALU.add, reads=[idb.b, idrep.b], writes=[idrep.b])
        op('dve', V.tensor_tensor, idrep.t[:], idrep.t[:], idb.t[:, 96:128], ALU.add, reads=[idb.b, idrep.b], writes=[idrep.b])
        Apw, BAw = getA()
        for r in range(4):
            for j in range(4):
                kw = {'tile_position': (0, 96)} if j == 3 else {}
                op('pe', PE_.matmul, Apw[32 * j:32 * j + 32, r * 32:(r + 1) * 32].rearrange("p (c g) -> p c g", c=4),
                   idb.t[:, 32 * r:32 * r + 32],
                   cwTb.t[:, :, :].rearrange("p c (g j) -> p c g j", j=4)[:, :, :, j], start=True, stop=True,
                   reads=[idb.b, cwTb.b], writes=[BAw], **kw)
        op('act', A_.copy, wsc.t[:].rearrange("p r c g -> p (r c g)"), Apw[:, 0:128], reads=[BAw], writes=[wsc.b])
        ck('su1')
        op('dve', V.tensor_tensor, omlb.t[:, 4:8], prm.t[:, 4:8], prm.t[:, 0:4], ALU.subtract,
           reads=[prm.b], writes=[omlb.b])
        op('act', A_.activation, omlb.t[:, 0:4], omlb.t[:, 4:8], AF.Sigmoid, reads=[omlb.b], writes=[omlb.b])
        op('act', A_.mul, omlb.t[:, 4:8], omlb.t[:, 0:4], -1.0, reads=[omlb.b], writes=[omlb.b])
        ck('su2')
        Aps, BA = getA()
        for hh in range(2):
            op('pe', PE_.matmul, Aps[:, hh * 512:(hh + 1) * 512], onerow.t[0:1, :], XT[0:1, 3, hh * 512:(hh + 1) * 512],
               start=True, stop=True, reads=[onerow.b, gfrow.b], writes=[BA])
        op('act', A_.copy, gfin.t[:], Aps[:, :], reads=[BA], writes=[gfin.b])

        def build_dg(lo, hi):
            for cg in range(lo // 18 * 3, min(16, (hi // 18) * 3)):
                ct, r = cg // 4, cg % 4
                e = 'dve' if cg % 2 == 0 else 'pool'
                op(e, fw.engs[e].tensor_tensor, WK[:, cg, :, :],
                   idrep.t[:, :].rearrange("p (o c) -> p o c", o=1).to_broadcast([128, 8, 32]),
                   wsc.t[:, r, ct, :].rearrange("p (g o) -> p g o", o=1).to_broadcast([128, 8, 32]), ALU.mult,
                   reads=[idrep.b, wsc.b], writes=[BWKa if e == 'dve' else BWKb])
        ck('su3')
        stag = [([W[i] for i in range(4)]), None]
        cast_engs = ['dve', 'pool', 'act']
        ci = 0
        for g in range(7):
            if g % 2 == 0:
                parts = [(W[i].t[:].rearrange("p a b -> p (a b)"), W[i].b) for i in range(4)]
            else:
                parts = [(XT[:, 2, :], Bx[2]), (XT[:, 3, :], Bx[3]),
                         (W[4].t[:].rearrange("p a b -> p (a b)"), W[4].b),
                         (W[5].t[:].rearrange("p a b -> p (a b)"), W[5].b)]
            for pi, (pap, pb) in enumerate(parts):
                src = w_in[pi * 256:(pi + 1) * 256, g * 512:(g + 1) * 512].rearrange("(k p) f -> p k f", p=128)
                fw.dma('sp', pap.rearrange("p (k f) -> p k f", k=2), src, pb, writes=[pb])
                for kk in range(2):
                    k = 2 * pi + kk
                    e = cast_engs[ci % 3]; ci += 1
                    dst = w_in_bf[:, k, g * 512:(g + 1) * 512]
                    srcv = pap[:, kk * 512:(kk + 1) * 512]
                    if e == 'act':
                        op('act', A_.activation, dst, srcv, AF.Copy, scale=prm.t[:, 8 + k:9 + k],
                           reads=[pb, prm.b], writes=[Bwin[g]])
                    elif e == 'pool':
                        op(e, G.tensor_scalar, dst, srcv, prm.t[:, 8 + k:9 + k], 1.0, ALU.mult, ALU.mult,
                           reads=[pb, prm.b], writes=[Bwin[g]])
                    else:
                        op(e, fw.engs[e].tensor_scalar, dst, srcv, prm.t[:, 8 + k:9 + k], None, ALU.mult,
                           reads=[pb, prm.b], writes=[Bwin[g]])
            build_dg(g * 18, (g + 1) * 18)
        ck('su4')
        for k in range(8):
            if k % 2 == 0:
                pap, pb = W[k // 2 % 4].t[:].rearrange("p a b -> p (a b)"), W[k // 2 % 4].b
            else:
                pap, pb = (XT[:, 2 + (k // 2) % 2, :], Bx[2 + (k // 2) % 2])
            fw.dma('sp', pap, w_out[k * 128:(k + 1) * 128, :], pb, writes=[pb])
            e = cast_engs[ci % 3]; ci += 1
            if e == 'act':
                op('act', A_.copy, w_out_bf[:, k, :], pap, reads=[pb], writes=[Bwout])
            else:
                op(e, fw.engs[e].tensor_copy, w_out_bf[:, k, :], pap, reads=[pb], writes=[Bwout])
        ck('su5')
        ck('setup')
        Bxs = [Buf("xs0"), Buf("xs1")]
        BstA = [Buf("stA0"), Buf("stA1")]
        BstE = Buf("stE")

        def stage_A1(T, tiles):
            for i, (xap, bx, P) in enumerate(tiles):
                op('act', A_.activation, xs.t[0:P, i, :], xap, AF.Square, accum_out=stt_.t[0:P, i:i + 1],
                   reads=[bx], writes=[Bxs[i], BstA[i]])
                op('pool', G.tensor_scalar, stt_.t[0:P, 4 + i:5 + i], stt_.t[0:P, i:i + 1], 1.0 / D, EPS, ALU.mult, ALU.add,
                   reads=[BstA[i]], writes=[BstA[i]])
                op('pool', G.tensor_tensor, stt_.t[0:P, 8 + i:9 + i], stt_.t[0:P, 4 + i:5 + i], cst.t[0:P, 1:2], ALU.pow,
                   reads=[BstA[i], cst.b], writes=[BstA[i]])
                op('pool', G.tensor_scalar, xs.t[0:P, i, :], xap, stt_.t[0:P, 8 + i:9 + i], 1.0, ALU.mult, ALU.mult,
                   reads=[bx, BstA[i]], writes=[Bxs[i]])

        def stage_A2(T, tiles):
            Aps, BA = getA()
            Abf = Aps[:].bitcast(BF16).rearrange("p (k t) -> p k t", k=8)
            for i, (xap, bx, P) in enumerate(tiles):
                for k in range(8):
                    op('pe', PE_.transpose, Abf[:, k, i * 128:i * 128 + P], xs.t[0:P, i, k * 128:(k + 1) * 128],
                       idb.t[0:P, 0:P], reads=[Bxs[i], idb.b], writes=[BA])
            op('act', A_.copy, hT.t[:, :, 0:T], Abf[:, :, 0:T], reads=[BA], writes=[hT.b])

        def stage_A(T, tiles):
            stage_A1(T, tiles)
            stage_A2(T, tiles)

        def inproj_fm(T, g):
            Aps, BA = getA()
            Av = Aps[:, 0:4 * T].rearrange("p (j t) -> p j t", j=4)
            for j in range(4):
                for k in range(8):
                    op('pe', PE_.matmul, Av[:, j, :], w_in_bf[:, k, g * 512 + j * 128:g * 512 + (j + 1) * 128],
                       hT.t[:, k, 0:T], start=(k == 0), stop=(k == 7), reads=[Bwin[g], hT.b], writes=[BA])
            return Av, BA

        def grp(name, T, ntile, P):
            ufp, sg = W[4], W[5]
            if name == 'f':
                Av, BA = inproj_fm(T, 1)
                op('act', A_.activation, W[1].t[:, :, 0:T], Av, AF.Sigmoid, scale=-1.0, reads=[BA], writes=[W[1].b])
            elif name == 'q':
                Av, BA = inproj_fm(T, 0)
                op('act', A_.activation, W[0].t[:, :, 0:T], Av, AF.Silu, reads=[BA], writes=[W[0].b])
            elif name == 'v':
                Aps, BA = getA()
                for i in range(ntile):
                    for k in range(8):
                        op('pe', PE_.matmul, Aps[0:P, i * 512:(i + 1) * 512], hT.t[:, k, i * 128:i * 128 + P],
                           w_in_bf[:, k, 1024:1536], start=(k == 0), stop=(k == 7), reads=[Bwin[2], hT.b], writes=[BA])
                op('dve', V.tensor_copy, vtok.t[0:P, 0:ntile, :], Aps[0:P, 0:ntile * 512].rearrange("p (i f) -> p i f", i=ntile),
                   reads=[BA], writes=[vtok.b])
            elif name == 'za':
                Av, BA = inproj_fm(T, 3)
                op('act', A_.activation, zas.t[:, :, 0:T], Av, AF.Silu, reads=[BA], writes=[zas.b])
            elif name == 'gate':
                Av, BA = inproj_fm(T, 5)
                for ct in range(4):
                    op('act', A_.activation, sg.t[:, ct, 0:T], Av[:, ct, :], AF.Sigmoid, bias=prm.t[:, 20 + ct:21 + ct],
                       reads=[BA, prm.b], writes=[sg.b])
            elif name == 'a':
                Av, BA = inproj_fm(T, 4)
                for ct in range(4):
                    op('dve', V.scalar_tensor_tensor, ufp.t[:, ct, 0:T], Av[:, ct, :], prm.t[:, 16 + ct:17 + ct], sg.t[:, ct, 0:T],
                       ALU.add, ALU.mult, reads=[BA, prm.b, sg.b], writes=[ufp.b])
            elif name == 'zb':
                Av, BA = inproj_fm(T, 6)
                op('act', A_.activation, zbs.t[:, :, 0:T], Av, AF.Silu, reads=[BA], writes=[zbs.b])

        def stage_B(T, ntile, P, ufp, sg):
            for name in ['f', 'q', 'v', 'za', 'gate', 'a', 'zb']:
                grp(name, T, ntile, P)

        def hgrn_pre(T, C, roff, ntile, P, phases='ABCD'):
            NC_ = T // C
            qs, k0, lf, rel, egp, egn = W[0], W[1], W[2], W[3], W[4], W[5]
            ng1 = NGp.t[:, :, 1:T + 1].rearrange("p h (c j) -> p h c j", j=C)
            ng0 = NGp.t[:, :, 0:T].rearrange("p h (c j) -> p h c j", j=C)
            Rr = ng0[:, :, :, roff:roff + 1]
            Ll = ng1[:, :, :, C - 1:C]
            Aa = ng0[:, :, :, 0:1]
            rel4 = rel.t[:, :, 0:T].rearrange("p h (c j) -> p h c j", j=C)
            lf4 = lf.t[:, :, 0:T].rearrange("p h (c j) -> p h c j", j=C)
            if 'A' in phases:
                for h in range(4):
                    op('act', A_.activation, lf.t[:, h, 0:T], k0.t[:, h, 0:T], AF.Ln, scale=omlb.t[:, 4 + h:5 + h], bias=1.0,
                       reads=[k0.b, omlb.b], writes=[lf.b])
            if 'B' in phases:
                for h in range(4):
                    op('dve', V.tensor_tensor_scan, NGp.t[:, h, 1:T + 1], cst.t[:, 0:1].to_broadcast([128, T]),
                       lf.t[:, h, 0:T], 0.0, ALU.mult, ALU.subtract, reads=[lf.b, cst.b], writes=[NGp.b])
                op('dve', V.tensor_tensor, lf4, Ll.to_broadcast([128, 4, NC_, C]), ng1, ALU.subtract,
                   reads=[NGp.b], writes=[lf.b])
                op('dve', V.tensor_tensor, rel4, ng1, Rr.to_broadcast([128, 4, NC_, C]), ALU.subtract,
                   reads=[NGp.b], writes=[rel.b])
                ddv = [dd.t[:, i, :, 0:NC_] for i in range(3)]
                sq = lambda a: a.rearrange("p h c o -> p h (c o)")
                op('dve', V.tensor_tensor, ddv[0], sq(Rr), sq(Aa), ALU.subtract, reads=[NGp.b], writes=[dd.b])
                op('dve', V.tensor_tensor, ddv[1], sq(Ll), sq(Rr), ALU.subtract, reads=[NGp.b], writes=[dd.b])
                op('dve', V.tensor_tensor, ddv[2], sq(Ll), sq(Aa), ALU.subtract, reads=[NGp.b], writes=[dd.b])
            if 'C' in phases:
                op('act', A_.activation, lf.t[:, :, 0:T], lf.t[:, :, 0:T], AF.Exp, scale=-1.0, reads=[lf.b], writes=[lf.b])
                op('act', A_.activation, egp.t[:, :, 0:T], rel.t[:, :, 0:T], AF.Exp, scale=-1.0, reads=[rel.b], writes=[egp.b])
                op('act', A_.activation, egn.t[:, :, 0:T], rel.t[:, :, 0:T], AF.Exp, reads=[rel.b], writes=[egn.b])
                op('act', A_.activation, ee.t[:, :, :, 0:NC_], dd.t[:, :, :, 0:NC_], AF.Exp, scale=-1.0, reads=[dd.b], writes=[ee.b])
            if 'D' in phases:
                for h in range(4):
                    op('dve', V.scalar_tensor_tensor, keT.t[:, h, 0:T], k0.t[:, h, 0:T], omlb.t[:, h:h + 1], lf.t[:, h, 0:T],
                       ALU.mult, ALU.mult, reads=[k0.b, omlb.b, lf.b], writes=[keT.b])
                op('dve', V.tensor_tensor, qT.t[:, :, 0:T], qs.t[:, :, 0:T], egp.t[:, :, 0:T], ALU.mult,
                   reads=[qs.b, egp.b], writes=[qT.b])
                for h in range(4):
                    op('dve', V.scalar_tensor_tensor, kT.t[:, h, 0:T], k0.t[:, h, 0:T], omlb.t[:, h:h + 1], egn.t[:, h, 0:T],
                       ALU.mult, ALU.mult, reads=[k0.b, omlb.b, egn.b], writes=[kT.b])

        def hgrn_pre2(T, ntile, P):
            PPb0 = PP[:, 0:256].bitcast(BF16)
            PPb1 = PP[:, 256:512].bitcast(BF16)
            for i in range(ntile):
                tgt, tb = (PPb0, BPP) if i == 0 else (PPb1, BPP)
                for h in range(4):
                    op('pe', PE_.transpose, tgt[0:P, h * 128:(h + 1) * 128], keT.t[:, h, i * 128:i * 128 + P], idb.t[:, :],
                       reads=[keT.b, idb.b], writes=[tb])
            for i in range(ntile):
                tgt, tb = (PPb0, BPP) if i == 0 else (PPb1, BPP)
                op('act', A_.copy, ktok.t[0:P, i, :], tgt[0:P, :], reads=[tb], writes=[ktok.b])

        post_state = {}

        def post_a(T):
            OTv = OT[:, 0:4 * TB].rearrange("p (h t) -> p h t", h=4)[:, :, 0:T]
            op('act', A_.activation, sqa.t[:, :, 0:T], OTv, AF.Square, reads=[BOT], writes=[sqa.b])

        def post_b(T):
            rstd = W[3]
            Aps, BA = getA()
            Av = Aps[:, 0:4 * T].rearrange("p (j t) -> p j t", j=4)
            for j in range(4):
                op('pe', PE_.matmul, Av[:, j, :], on128.t[:, :], sqa.t[:, j, 0:T], start=True, stop=True,
                   reads=[on128.b, sqa.b], writes=[BA])
            op('act', A_.activation, rstd.t[:, :, 0:T], Av, AF.Ln, bias=cst.t[:, 2:3], reads=[BA, cst.b], writes=[rstd.b])
            op('act', A_.activation, rstd.t[:, :, 0:T], rstd.t[:, :, 0:T], AF.Exp, scale=-0.5, reads=[rstd.b], writes=[rstd.b])

        def post_c(T):
            OTv = OT[:, 0:4 * TB].rearrange("p (h t) -> p h t", h=4)[:, :, 0:T]
            rstd = W[3]
            oa = rstd
            op('dve', V.scalar_tensor_tensor, oa.t[:, :, 0:T], OTv, prm.t[:, 36:37], rstd.t[:, :, 0:T], ALU.mult, ALU.mult,
               reads=[BOT, prm.b, rstd.b], writes=[oa.b])
            op('dve', V.tensor_tensor, oT.t[:, 0:4, 0:T], oa.t[:, :, 0:T], zas.t[:, :, 0:T], ALU.mult,
               reads=[oa.b, zas.b], writes=[oT.b])

        def hgrn_post(T):
            post_a(T)
            post_b(T)
            post_c(T)

        def conv_mm(T):
            Aps, BA = getA()
            Av = Aps[:, 0:4 * T].rearrange("p (j t) -> p j t", j=4)
            for ct in range(4):
                for jg in range(8):
                    for q in range(4):
                        cg = 4 * ct + q
                        kw = {'tile_position': (0, 96)} if q == 3 else {}
                        op('pe', PE_.matmul, Av[32 * q:32 * q + 32, ct, :], WK[:, cg, jg, :], UBk.t[:, cg, 4 * jg:4 * jg + T],
                           start=(jg == 0), stop=(jg == 7), reads=[BWKa, BWKb, UBk.b], writes=[BA], **kw)
            return Av, BA

        def conv_dve_ops(T):
            acc = W[2]
            accb = [Buf("acc%d" % ct) for ct in range(4)]
            ops = []
            for jj in range(31):
                for ct in range(4):
                    av = acc.t[:, ct, 0:T].rearrange("p (s t) -> p s t", t=4)
                    wr = [accb[ct]] + ([acc.b] if jj == 0 else [])
                    if jj == 0:
                        ops.append(lambda av=av, ct=ct, wr=wr: op('dve', V.tensor_scalar, av, UBs.t[:, ct, :, 0:4], cwT.t[:, ct, 1:2], None,
                                                                  ALU.mult, reads=[UBs.b, cwT.b], writes=wr))
                    else:
                        ops.append(lambda av=av, ct=ct, jj=jj, wr=wr: op('dve', V.scalar_tensor_tensor, av, UBs.t[:, ct, :, jj:jj + 4],
                                                                         cwT.t[:, ct, 1 + jj:2 + jj], av, ALU.mult, ALU.add,
                                                                         reads=[UBs.b, cwT.b], writes=wr))
            return ops, acc.t[:, :, 0:T], [acc.b] + accb

        def ln1(T, Av, BA, mid=None, defer=False):
            cvb = W[3]
            for ct in range(4):
                op('act', A_.activation, cvb.t[:, ct, 0:T], Av[:, ct, :], AF.Identity, bias=prm.t[:, 24 + ct:25 + ct],
                   reads=[BA, prm.b], writes=[cvb.b])
                op('act', A_.activation, sqa.t[:, ct, 0:T], Av[:, ct, :], AF.Square, bias=prm.t[:, 24 + ct:25 + ct],
                   reads=[BA, prm.b], writes=[sqa.b])
            if mid is not None:
                mid()
            if defer:
                return None, None
            return ln1b(T)

        def ln1b(T):
            cvb = W[3]
            Aps2, BA2 = getA()
            for ct in range(4):
                op('pe', PE_.matmul, Aps2[:, 0:T], on512f.t[:, :], cvb.t[:, ct, 0:T], start=(ct == 0), stop=(ct == 3),
                   reads=[on512f.b, cvb.b], writes=[BA2])
            for ct in range(4):
                op('pe', PE_.matmul, Aps2[:, 512:512 + T], on512.t[:, :], sqa.t[:, ct, 0:T], start=(ct == 0), stop=(ct == 3),
                   reads=[on512.b, sqa.b], writes=[BA2])
            return Aps2, BA2

        def ln2(T, Aps2, BA2):
            mean_ps = Aps2[:, 0:T]
            msq_ps = Aps2[:, 512:512 + T]
            op('act', A_.activation, sml.t[:, 0:T], mean_ps, AF.Square, reads=[BA2], writes=[sml.b])
            op('act', A_.copy, mean_sb.t[:, 0:T], mean_ps, reads=[BA2], writes=[mean_sb.b])
            op('dve', V.tensor_tensor, sml.t[:, 0:T], msq_ps, sml.t[:, 0:T], ALU.subtract, reads=[BA2, sml.b], writes=[sml.b])
            op('act', A_.activation, sml.t[:, 0:T], sml.t[:, 0:T], AF.Ln, bias=cst.t[:, 2:3], reads=[sml.b, cst.b], writes=[sml.b])
            op('act', A_.activation, rsl.t[:, 0:T], sml.t[:, 0:T], AF.Exp, scale=-0.5, reads=[sml.b], writes=[rsl.b])

        def ln3(T, Aps2, BA2):
            cvb = W[3]
            op('dve', V.tensor_tensor, cvb.t[:, :, 0:T], cvb.t[:, :, 0:T],
               mean_sb.t[:, 0:T].rearrange("p (o t) -> p o t", o=1).to_broadcast([128, 4, T]), ALU.subtract,
               reads=[mean_sb.b, cvb.b], writes=[cvb.b])
            op('dve', V.tensor_tensor, cvb.t[:, :, 0:T], cvb.t[:, :, 0:T],
               rsl.t[:, 0:T].rearrange("p (o t) -> p o t", o=1).to_broadcast([128, 4, T]), ALU.mult,
               reads=[rsl.b, cvb.b], writes=[cvb.b])

        def ln4(T):
            cvb = W[3]
            for ct in range(4):
                op('act', A_.activation, cvy.t[:, ct, 0:T], cvb.t[:, ct, 0:T], AF.Silu, scale=prm.t[:, 28 + ct:29 + ct],
                   bias=prm.t[:, 32 + ct:33 + ct], reads=[cvb.b, prm.b], writes=[cvy.b])
            op('dve', V.tensor_tensor, oT.t[:, 4:8, 0:T], cvy.t[:, :, 0:T], zbs.t[:, :, 0:T], ALU.mult,
               reads=[cvy.b, zbs.b], writes=[oT.b])

        def conv_ln_sample(T, Av, BA):
            Aps2, BA2 = ln1(T, Av, BA)
            ln2(T, Aps2, BA2)
            ln3(T, Aps2, BA2)
            ln4(T)

        e_state = {}

        def stage_E(tiles, dsts, phases='01234'):
            if '0' in phases:
                slots_e = []
                for i, (xap, bx, P) in enumerate(tiles):
                    if i == 0:
                        Aps, BA = getA()
                    else:
                        Aps, BA = OT, BOT
                    for hh in range(2):
                        for k in range(8):
                            op('pe', PE_.matmul, Aps[0:P, hh * 512:(hh + 1) * 512], oT.t[:, k, i * 128:i * 128 + P],
                               w_out_bf[:, k, hh * 512:(hh + 1) * 512], start=(k == 0), stop=(k == 7),
                               reads=[oT.b, Bwout], writes=[BA])
                    slots_e.append((Aps, BA))
                e_state['slots'] = slots_e
            slots_e = e_state['slots']
            if '1' in phases:
                for i, (xap, bx, P) in enumerate(tiles):
                    Aps, BA = slots_e[i]
                    op('dve', V.tensor_tensor, xap, Aps[0:P, :], xap, ALU.add, reads=[BA, bx], writes=[bx])
            if '2' in phases:
                for i, (xap, bx, P) in enumerate(tiles):
                    op('act', A_.activation, sqa.t[0:P, :, :].rearrange("p a b -> p (a b)"), xap, AF.Square,
                       accum_out=stt_.t[0:P, 12 + i:13 + i], reads=[bx], writes=[sqa.b, BstE])
            if '3' in phases:
                for i, (xap, bx, P) in enumerate(tiles):
                    op('pool', G.tensor_scalar, stt_.t[0:P, 12 + i:13 + i], stt_.t[0:P, 12 + i:13 + i], 1.0 / D, EPS, ALU.mult, ALU.add,
                       reads=[BstE], writes=[BstE])
                    op('pool', G.tensor_tensor, stt_.t[0:P, 14 + i:15 + i], stt_.t[0:P, 12 + i:13 + i], cst.t[0:P, 1:2], ALU.pow,
                       reads=[BstE, cst.b], writes=[BstE])
            if '4' in phases:
                for i, (xap, bx, P) in enumerate(tiles):
                    op('dve', V.scalar_tensor_tensor, xap, xap, stt_.t[0:P, 14 + i:15 + i], gfin.t[0:P, :], ALU.mult, ALU.mult,
                       reads=[bx, BstE, gfin.b], writes=[bx])
                    fw.dma('sp', dsts[i], xap, bx, reads=[bx])
                    if bx not in out_bufs:
                        out_bufs.append(bx)

        with ExitStack() as es_s:
            S0 = [TT("S0_%d" % i, [128, 4, 128], F32) for i in range(2)]
            S0bf = [TT("S0bf_%d" % i, [128, 4, 128], BF16) for i in range(2)]
            UBs = TT("UBs", [128, 4, NSS, 34], BF16)
            kem = [TT("kem_%d" % i, [64, 512], BF16) for i in range(2)]
            UB = UBs
            T = TS
            for g4 in range(4):
                fw.dma('sp', XT[0:120, 1 + g4 // 2, (g4 % 2) * 512:(g4 % 2 + 1) * 512],
                       st_c[g4 * 4:(g4 + 1) * 4, :, :].rearrange("s j c -> (s j) c"), Bx[1 + g4 // 2], writes=[Bx[1 + g4 // 2]])
            Bd2d = Buf("d2d")
            fw.dma('sp', sc_s[:, 0:26, :], st_c[:, 4:30, :], Bd2d)
            out_bufs.append(Bd2d)
            for ct in range(4):
                Aps, BA = getA()
                for g4 in range(4):
                    op('pe', PE_.transpose, Aps[:, g4 * 120:(g4 + 1) * 120],
                       XT[0:120, 1 + g4 // 2, (g4 % 2) * 512 + ct * 128:(g4 % 2) * 512 + (ct + 1) * 128],
                       idf.t[0:120, 0:120], reads=[Bx[1 + g4 // 2], idf.b], writes=[BA])
                op('act', A_.copy, UBs.t[:, ct, :, 0:30], Aps[:, 0:480].rearrange("p (s j) -> p s j", j=30),
                   reads=[BA], writes=[UBs.b])

            ck('s_hist')
            tiles = [(XT[0:TS, 0, :], Bx[0], TS)]
            stage_A(T, tiles)
            ck('s_A')
            ufp, sg = W[4], W[5]
            stage_B(T, 1, TS, ufp, sg)
            ck('s_B')
            for ct in range(4):
                op('pool', G.tensor_copy, UBs.t[:, ct, :, 30:34], ufp.t[:, ct, 0:T].rearrange("p (s t) -> p s t", t=4),
                   reads=[ufp.b], writes=[UBs.b])
            Aps, BA = getA()
            for ct in range(4):
                op('pe', PE_.transpose, Aps[0:TS, ct * 128:(ct + 1) * 128], ufp.t[:, ct, 0:T], idf.t[:, :],
                   reads=[ufp.b, idf.b], writes=[BA])
            op('act', A_.copy, utok.t[0:TS, :], Aps[0:TS, 0:512], reads=[BA], writes=[utok.b])
            for s in range(NSS):
                fw.dma('sp', sc_s[s, 26:30, :], utok.t[s * 4:(s + 1) * 4, :], utok.b, reads=[utok.b])
            out_bufs.append(utok.b)

            ck('s_u')
            hgrn_pre(T, 4, 0, 1, TS)
            hgrn_pre2(T, 1, TS)
            cv_ops, cvAv, cvBA = conv_dve_ops(T)
            ck('s_pre')
            for h in range(4):
                op('pe', PE_.matmul, SC[0:TS, h * 64:(h + 1) * 64], kT.t[:, h, 0:T], qT.t[:, h, 0:T], start=True, stop=True,
                   reads=[kT.b, qT.b], writes=[BSC])
            ck('h0')
            op('dve', V.scalar_tensor_tensor, ssb.t[0:TS, 0:256], SC[0:TS, :], BIG, masks.t[0:TS, :], ALU.min, ALU.mult,
               reads=[BSC, masks.b], writes=[ssb.b])
            ck('h1')
            for h in range(4):
                op('pe', PE_.matmul, OT[:, h * TB:h * TB + T], vtok.t[0:TS, 0, h * 128:(h + 1) * 128],
                   ssb.t[0:TS, h * 64:(h + 1) * 64], start=(h % 2 == 0), stop=False, skip_group_check=True,
                   reads=[vtok.b, ssb.b], writes=[BOT])
            ck('h2')
            BtS = [Buf("tS0"), Buf("tS1")]
            fw.dma('sp', S0[0].t[:], st_h[0].rearrange("h d v -> d h v"), S0[0].b, writes=[S0[0].b])
            for s in range(NSS):
                sl = s % 2
                for _ in range(8):
                    if cv_ops:
                        cv_ops.pop(0)()
                if s + 1 < NSS:
                    fw.dma('sp', S0[1 - sl].t[:], st_h[s + 1].rearrange("h d v -> d h v"), S0[1 - sl].b, writes=[S0[1 - sl].b])
                op('act', A_.copy, S0bf[sl].t[:], S0[sl].t[:], reads=[S0[sl].b], writes=[S0bf[sl].b])
                if s == 0: ck('h3')
                for h in range(4):
                    op('pe', PE_.matmul, OT[:, h * TB + s * 4:h * TB + s * 4 + 4], S0bf[sl].t[:, h, :],
                       qT.t[:, h, s * 4:s * 4 + 4], start=False, stop=(s == NSS - 1), skip_group_check=True,
                       reads=[S0bf[sl].b, qT.b], writes=[BOT])
                if s == 0: ck('h4')
                if s == 0:
                    op('pool', G.tensor_scalar, kem[0].t[:, :], ktok.t[0:TS, 0, :], mcol.t[:, 0:1], 1.0, ALU.mult, ALU.mult,
                       reads=[ktok.b, mcol.b], writes=[kem[0].b])
                if s + 1 < NSS:
                    op('pool', G.tensor_scalar, kem[1 - sl].t[:, :], ktok.t[0:TS, 0, :], mcol.t[:, s + 1:s + 2], 1.0, ALU.mult, ALU.mult,
                       reads=[ktok.b, mcol.b], writes=[kem[1 - sl].b])
                if s == 0: ck('h5')
                ppap, ppb = (PP, BPP) if sl == 0 else (MISC, BSC)
                for h in range(4):
                    op('pe', PE_.matmul, ppap[:, h * 128:(h + 1) * 128], kem[sl].t[:, h * 128:(h + 1) * 128],
                       vtok.t[0:TS, 0, h * 128:(h + 1) * 128], start=True, stop=True,
                       reads=[kem[sl].b, vtok.b], writes=[ppb])
                if s == 0: ck('h6')
                tSl = XT[:, 3, sl * 512:(sl + 1) * 512].rearrange("p (h v) -> p h v", h=4)
                wr = [BtS[sl]] + ([Bx[3]] if s < 2 else [])
                for h in range(4):
                    op('dve', V.scalar_tensor_tensor, tSl[:, h, :], S0[sl].t[:, h, :], ee.t[:, 2, h, s:s + 1],
                       ppap[:, h * 128:(h + 1) * 128], ALU.mult, ALU.add, reads=[S0[sl].b, ee.b, ppb], writes=wr)
                if s == 0: ck('h8')
                fw.dma('pool', sh_s[s].rearrange("h d v -> d h v"), tSl, BtS[sl], reads=[BtS[sl]])
            ck('s_hgrn')
            while cv_ops:
                cv_ops.pop(0)()
            conv_ln_sample(T, cvAv, cvBA)
            hgrn_post(T)
            ck('s_post')
            stage_E(tiles, [y_s[:, :]])
            ck('s_E')

        with ExitStack() as es_p:
            S = S0[0]
            Sbf2 = [S0bf[0], S0bf[1]]
            T = TB
            op('pool', G.memset, S.t[:], 0.0, writes=[S.b])
            op('pool', G.memset, Ue.t[:, :, 0:31], 0.0, writes=[Ue.b])

            def xtiles(b):
                base = ((b + 1) % 3) * 2
                return [(XT[:, base + i, :], Bx[base + i], 128) for i in range(2)]

            def xload(b):
                for i, (xap, bx, P) in enumerate(xtiles(b)):
                    extra = [BtS[0], BtS[1]] if (b == 0 and i == 1) else []
                    fw.dma('sp', xap, x_p[b * TB + i * 128:b * TB + (i + 1) * 128, :], bx, writes=[bx] + extra)

            def after_a(bb):
                ufp = W[4]
                if bb > 0:
                    op('pool', G.tensor_copy, Ue.t[:, :, 0:31], Ue.t[:, :, T:T + 31], reads=[Ue.b], writes=[Ue.b])
                op('pool', G.tensor_copy, Ue.t[:, :, 31:31 + T], ufp.t[:, :, 0:T], reads=[ufp.b], writes=[Ue.b])
                for j in range(4):
                    for r in range(4):
                        fw.dma('sp', UBk.t[32 * j:32 * j + 32, r:16:4, :], Ue.t[32 * r:32 * r + 32, :, j:j + T + 28],
                               UBk.b, reads=[Ue.b], writes=[UBk.b], nowaw=True)
                if bb == NBLK - 1:
                    Aps, BA = getA()
                    for ct in range(4):
                        op('pe', PE_.transpose, Aps[0:30, ct * 128:(ct + 1) * 128], ufp.t[:, ct, T - 30:T], idf.t[:, :],
                           reads=[ufp.b, idf.b], writes=[BA])
                    op('act', A_.copy, utok.t[0:30, :], Aps[0:30, 0:512], reads=[BA], writes=[utok.b])
                    fw.dma('sp', sc_p[:, :], utok.t[0:30, :], utok.b, reads=[utok.b])

            xload(0)
            if NBLK > 1:
                xload(1)
            stage_A(T, xtiles(0))
            for name in ['f', 'q', 'gate', 'a']:
                grp(name, T, 2, 128)
            after_a(0)
            grp('zb', T, 2, 128)
            for b in range(NBLK):
                nxt = b + 1 < NBLK
                tiles = xtiles(b)
                if b == 0:
                    if b + 2 < NBLK:
                        xload(b + 2)
                    hgrn_pre(T, 64, 32, 2, 128, 'A')
                    if nxt:
                        stage_A1(T, xtiles(b + 1))
                    hgrn_pre(T, 64, 32, 2, 128, 'B')
                    grp('za', T, 2, 128)
                    grp('v', T, 2, 128)
                    hgrn_pre(T, 64, 32, 2, 128, 'CD')
                else:
                    prev_tiles, prev_dsts = pending_E
                    hgrn_pre(T, 64, 32, 2, 128, 'A')
                    hgrn_pre(T, 64, 32, 2, 128, 'B')
                    hgrn_pre(T, 64, 32, 2, 128, 'C')
                    stage_E(prev_tiles, prev_dsts, '1')
                    stage_E(prev_tiles, prev_dsts, '23')
                    hgrn_pre(T, 64, 32, 2, 128, 'D')
                    stage_E(prev_tiles, prev_dsts, '4')
                    if b + 2 < NBLK:
                        xload(b + 2)
                Avc, BAc = conv_mm(T)
                ln1(T, Avc, BAc, mid=(lambda: stage_A2(T, xtiles(b + 1))) if nxt else None, defer=True)
                hgrn_pre2(T, 2, 128)
                NCH = T // 64
                SCF = MISC[:, 0:512]
                for c in range(NCH):
                    po = (c % 2) * 64
                    cs = slice(c * 64, (c + 1) * 64)
                    for h in range(4):
                        op('pe', PE_.matmul, SCF[po:po + 64, (c // 2) * 256 + h * 64:(c // 2) * 256 + (h + 1) * 64],
                           kT.t[:, h, cs], qT.t[:, h, cs], start=True, stop=True, reads=[kT.b, qT.b], writes=[BSC])
                for o in range(2):
                    op('dve', V.scalar_tensor_tensor, ssb.t[:, o * 256:(o + 1) * 256], SCF[:, o * 256:(o + 1) * 256], BIG,
                       maskp.t[:, :], ALU.min, ALU.mult, reads=[BSC, maskp.b], writes=[ssb.b])
                pp_of = [(PP[:, :], BPP), (MISC[:, :], BSC), (PP[:, :], BPP), (MISC[:, :], BSC)]

                def emit_P(c):
                    i = c // 2
                    po = (c % 2) * 64
                    pap, pb = pp_of[c]
                    for h in range(4):
                        op('pe', PE_.matmul, pap[:, h * 128:(h + 1) * 128], ktok.t[po:po + 64, i, h * 128:(h + 1) * 128],
                           vtok.t[po:po + 64, i, h * 128:(h + 1) * 128], start=True, stop=True,
                           reads=[ktok.b, vtok.b], writes=[pb])
                Aps2, BA2 = ln1b(T)
                ln2(T, Aps2, BA2)
                emit_P(0)
                emit_P(1)
                if nxt:
                    grp('f', T, 2, 128)
                    grp('q', T, 2, 128)
                for c in range(NCH):
                    i = c // 2
                    po = (c % 2) * 64
                    cs = slice(c * 64, (c + 1) * 64)
                    pap, pb = pp_of[c]
                    Sbf = Sbf2[c % 2]
                    op('dve', V.tensor_tensor, Sbf.t[:], S.t[:], ee.t[:, 0, :, c:c + 1].to_broadcast([128, 4, 128]), ALU.mult,
                       reads=[S.b, ee.b], writes=[Sbf.b])
                    for h in range(4):
                        op('dve', V.scalar_tensor_tensor, S.t[:, h, :], S.t[:, h, :], ee.t[:, 2, h, c:c + 1],
                           pap[:, h * 128:(h + 1) * 128], ALU.mult, ALU.add, reads=[ee.b, pb], writes=[S.b])
                    if c + 2 < NCH:
                        emit_P(c + 2)
                    for h in range(4):
                        oap = OT[:, h * TB + c * 64:h * TB + (c + 1) * 64]
                        op('pe', PE_.matmul, oap, vtok.t[po:po + 64, i, h * 128:(h + 1) * 128],
                           ssb.t[po:po + 64, (c // 2) * 256 + h * 64:(c // 2) * 256 + (h + 1) * 64], start=True, stop=False,
                           reads=[vtok.b, ssb.b], writes=[BOT])
                        op('pe', PE_.matmul, oap, Sbf.t[:, h, :], qT.t[:, h, cs], start=False, stop=True,
                           reads=[Sbf.b, qT.b], writes=[BOT])
                    if c == 1:
                        ln3(T, Aps2, BA2)
                        if b + 2 < NBLK:
                            stage_A1(T, xtiles(b + 2))
                post_a(T)
                if nxt:
                    grp('gate', T, 2, 128)
                    grp('a', T, 2, 128)
                    after_a(b + 1)
                ln4(T)
                post_b(T)
                post_c(T)
                if nxt:
                    grp('zb', T, 2, 128)
                    grp('za', T, 2, 128)
                    grp('v', T, 2, 128)
                dsts_b = [y_p[b * TB + i * 128:b * TB + (i + 1) * 128, :] for i in range(2)]
                if nxt:
                    stage_E(tiles, dsts_b, '0')
                    pending_E = (tiles, dsts_b)
                else:
                    stage_E(tiles, dsts_b)
            fw.dma('sp', sh_p.rearrange("h d v -> d h v"), S.t[:], S.b, reads=[S.b])
            out_bufs.append(S.b)
            fw.wait_all('sp', out_bufs)
            fw.barrier()


_NC_CACHE = {}


def _consts():
    idf = np.eye(128, dtype=np.float32)
    idb = idf.astype(ml_dtypes.bfloat16)
    s = np.arange(64)[:, None]
    t = np.arange(64)[None, :]
    mp = (s <= t).astype(np.float32)
    maskp = np.tile(np.tile(mp, (1, 4)), (2, 1))
    ms = ((s // 4 == t // 4) & (s <= t)).astype(np.float32)
    masks = np.tile(np.tile(ms, (1, 4)), (2, 1))
    mcol = (np.arange(64)[:, None] // 4 == np.arange(16)[None, :]).astype(np.float32)
    return idb, idf, np.ascontiguousarray(maskp), np.ascontiguousarray(masks), np.ascontiguousarray(mcol)


def kernel(x_prompt, x_sample, state_hgrn, state_conv, norm_in_g, w_in, lb_logits, hgrn_norm_g,
           b_glu, conv_w, conv_b, ln_g, ln_b, w_out, final_norm_g):
    f = lambda a: np.ascontiguousarray(np.asarray(a, dtype=np.float32))
    if 'nc' not in _NC_CACHE:
        _NC_CACHE['nc'] = build_nc()
    nc = _NC_CACHE['nc']
    idb, idf, maskp, masks, mcol = _consts()
    x_prompt = f(x_prompt); x_sample = f(x_sample); state_hgrn = f(state_hgrn); state_conv = f(state_conv)
    shared = {
        "norm_in_g": f(norm_in_g).reshape(8, 128), "w_in": f(w_in).reshape(D, INW),
        "lb_logits": f(lb_logits).reshape(8, 128), "hgrn_norm_g": f(hgrn_norm_g).reshape(1, 128),
        "b_glu": f(b_glu).reshape(8, 128), "conv_w": f(conv_w).reshape(31, 512),
        "conv_b": f(conv_b).reshape(4, 128), "ln_g": f(ln_g).reshape(4, 128), "ln_b": f(ln_b).reshape(4, 128),
        "w_out": f(w_out).reshape(D, D), "final_g": f(final_norm_g).reshape(1, D),
        "c_idb": idb, "c_idf": idf, "c_maskp": maskp, "c_masks": masks, "c_mcol": mcol,
    }
    in_maps = []
    for c in range(NCORES):
        m = dict(shared)
        m["x_p"] = np.ascontiguousarray(x_prompt[c])
        m["x_s"] = np.ascontiguousarray(x_sample[c * NSS:(c + 1) * NSS].reshape(TS, D))
        m["st_h"] = np.ascontiguousarray(state_hgrn[0, c * NSS:(c + 1) * NSS])
        m["st_c"] = np.ascontiguousarray(state_conv[0, c * NSS:(c + 1) * NSS])
        in_maps.append(m)
    res = run_bass_kernel_spmd(nc, in_maps, core_ids=list(range(NCORES)))
    R = res.results
    y_prompt = np.stack([R[c]["y_p"] for c in range(NCORES)], 0)
    y_sample = np.concatenate([R[c]["y_s"].reshape(NSS, 4, D) for c in range(NCORES)], 0)
    shp = np.stack([R[c]["sh_p"] for c in range(NCORES)], 0)[None]
    scp = np.stack([R[c]["sc_p"] for c in range(NCORES)], 0)[None]
    shs = np.concatenate([R[c]["sh_s"] for c in range(NCORES)], 0)[None]
    scs = np.concatenate([R[c]["sc_s"] for c in range(NCORES)], 0)[None]
    return (y_prompt.astype(np.float32), y_sample.astype(np.float32), shp.astype(np.float32),
            scp.astype(np.float32), shs.astype(np.float32), scs.astype(np.float32))
```
